# Optimizing a Trainium2 kernel written in Bass

```python
import math
import jax, jax.numpy as jnp
from jax import lax
import numpy as np

D_MODEL = 2048
BATCH = 2
SEQ = 8192
DEPTH = 1

MIX_WIDTH = D_MODEL
CONV_WIDTH = MIX_WIDTH // 2
CONV_GROUPS = 8
MLA_HEADS = 8
QK_NOPE_DIM = 128
QK_ROPE_DIM = 64
V_HEAD_DIM = 128
Q_LORA_RANK = 768
KV_LORA_RANK = 512
ROPE_THETA = 10000.0
Q_BLOCK = 128
D_FF = 5632
CONV_K = 3
RMS_EPS = 1e-6
N_MOD = 6

IN_SPLITS = (Q_LORA_RANK, KV_LORA_RANK, QK_ROPE_DIM, CONV_WIDTH, CONV_WIDTH, CONV_WIDTH)
IN_COLS = sum(IN_SPLITS)

kernel_name = "hybrid_mla_shortconv_convffn_adaln"


def rms_norm(x, g):
    xf = x.astype(jnp.float32)
    y = xf * lax.rsqrt(jnp.mean(xf * xf, axis=-1, keepdims=True) + RMS_EPS)
    return (y * g.astype(jnp.float32)).astype(x.dtype)


def rope(x, cos, sin):
    x1, x2 = jnp.split(x, 2, axis=-1)
    return jnp.concatenate([x1 * cos - x2 * sin, x2 * cos + x1 * sin], axis=-1)


def causal_dwconv3(u, w, b):
    s = u.shape[1]
    up = jnp.pad(u, ((0, 0), (CONV_K - 1, 0), (0, 0)))
    return up[:, :s] * w[0] + up[:, 1:s + 1] * w[1] + u * w[2] + b


def mla_attention(q_nope, q_rope, k_nope, k_rope, v):
    b, s, h, _ = q_nope.shape
    nb = s // Q_BLOCK
    scale = 1.0 / math.sqrt(QK_NOPE_DIM + QK_ROPE_DIM)
    k_idx = jnp.arange(s)
    neg = jnp.finfo(jnp.float32).min

    def blockify(t):
        return t.reshape(b, nb, Q_BLOCK, *t.shape[2:]).swapaxes(0, 1)

    def one_block(args):
        qn, qr, i = args
        sc = (jnp.einsum('bqhd,bkhd->bhqk', qn, k_nope)
              + jnp.einsum('bqhd,bkd->bhqk', qr, k_rope)).astype(jnp.float32) * scale
        q_idx = i * Q_BLOCK + jnp.arange(Q_BLOCK)
        mask = k_idx[None, :] <= q_idx[:, None]
        sc = jnp.where(mask, sc, neg)
        p = jax.nn.softmax(sc, axis=-1).astype(v.dtype)
        return jnp.einsum('bhqk,bkhd->bqhd', p, v)

    out = lax.map(one_block, (blockify(q_nope), blockify(q_rope), jnp.arange(nb)))
    return out.swapaxes(0, 1).reshape(b, s, h * V_HEAD_DIM)


def setup_inputs(seed: int = 0) -> dict:
    key = jax.random.key(seed)
    ks = jax.random.split(key, 24)
    f32 = jnp.float32

    def nrm(k, shape, fan_in):
        return jax.random.normal(k, shape, f32) * (fan_in ** -0.5)

    def gain(k, n):
        return 1.0 + 0.02 * jax.random.normal(k, (DEPTH, n), f32)

    x = jax.random.normal(ks[0], (BATCH, SEQ, D_MODEL), f32)
    c = jax.random.normal(ks[1], (BATCH, D_MODEL), f32)
    offset = jax.random.randint(ks[2], (BATCH, 1), 0, 1024, dtype=jnp.int32)
    positions = offset + jnp.arange(SEQ, dtype=jnp.int32)[None, :]
    return {
        "x": x,
        "c": c,
        "positions": positions,
        "w_ada": nrm(ks[3], (DEPTH, D_MODEL, N_MOD * D_MODEL), D_MODEL),
        "b_ada": 0.02 * jax.random.normal(ks[4], (DEPTH, N_MOD * D_MODEL), f32),
        "g_pre_mix": gain(ks[5], D_MODEL),
        "g_post_mix": gain(ks[6], D_MODEL),
        "w_in": nrm(ks[7], (DEPTH, D_MODEL, IN_COLS), D_MODEL),
        "g_q": gain(ks[8], Q_LORA_RANK),
        "w_uq": nrm(ks[9], (DEPTH, Q_LORA_RANK, MLA_HEADS * (QK_NOPE_DIM + QK_ROPE_DIM)), Q_LORA_RANK),
        "g_kv": gain(ks[10], KV_LORA_RANK),
        "w_ukv": nrm(ks[11], (DEPTH, KV_LORA_RANK, MLA_HEADS * (QK_NOPE_DIM + V_HEAD_DIM)), KV_LORA_RANK),
        "conv_w_mix": nrm(ks[12], (DEPTH, CONV_K, CONV_WIDTH), CONV_K),
        "conv_b_mix": 0.02 * jax.random.normal(ks[13], (DEPTH, CONV_WIDTH), f32),
        "w_o": nrm(ks[14], (DEPTH, MIX_WIDTH, D_MODEL), MIX_WIDTH),
        "g_pre_ffn": gain(ks[15], D_MODEL),
        "g_post_ffn": gain(ks[16], D_MODEL),
        "w_up": nrm(ks[17], (DEPTH, D_MODEL, 2 * D_FF), D_MODEL),
        "conv_w_ffn": nrm(ks[18], (DEPTH, CONV_K, 2 * D_FF), CONV_K),
        "conv_b_ffn": 0.02 * jax.random.normal(ks[19], (DEPTH, 2 * D_FF), f32),
        "w_down": nrm(ks[20], (DEPTH, D_FF, D_MODEL), D_FF),
    }


def reference(x, c, positions, w_ada, b_ada, g_pre_mix, g_post_mix, w_in, g_q, w_uq,
              g_kv, w_ukv, conv_w_mix, conv_b_mix, w_o, g_pre_ffn, g_post_ffn,
              w_up, conv_w_ffn, conv_b_ffn, w_down):
    b, s, _ = x.shape
    inv_freq = 1.0 / (ROPE_THETA ** (jnp.arange(0, QK_ROPE_DIM, 2, dtype=jnp.float32) / QK_ROPE_DIM))
    ang = positions.astype(jnp.float32)[..., None] * inv_freq
    cos = jnp.cos(ang).astype(x.dtype)
    sin = jnp.sin(ang).astype(x.dtype)
    c_act = jax.nn.silu(c)
    cut = np.cumsum(IN_SPLITS)[:-1].tolist()

    for l in range(DEPTH):
        mod = c_act @ w_ada[l] + b_ada[l]
        sh_m, sc_m, gt_m, sh_f, sc_f, gt_f = [m[:, None, :] for m in jnp.split(mod, N_MOD, axis=-1)]

        h = rms_norm(x, g_pre_mix[l]) * (1.0 + sc_m) + sh_m
        proj = h @ w_in[l]
        q_lat, kv_lat, k_rope, gate_b, gate_c, conv_in = jnp.split(proj, cut, axis=-1)

        q = (rms_norm(q_lat, g_q[l]) @ w_uq[l]).reshape(b, s, MLA_HEADS, QK_NOPE_DIM + QK_ROPE_DIM)
        q_nope, q_rope = q[..., :QK_NOPE_DIM], q[..., QK_NOPE_DIM:]
        q_rope = rope(q_rope, cos[:, :, None, :], sin[:, :, None, :])
        k_rope = rope(k_rope, cos, sin)
        kv = (rms_norm(kv_lat, g_kv[l]) @ w_ukv[l]).reshape(b, s, MLA_HEADS, QK_NOPE_DIM + V_HEAD_DIM)
        k_nope, v = kv[..., :QK_NOPE_DIM], kv[..., QK_NOPE_DIM:]
        attn_out = mla_attention(q_nope, q_rope, k_nope, k_rope, v)

        conv_out = gate_b * causal_dwconv3(gate_c * conv_in, conv_w_mix[l], conv_b_mix[l])

        mix = jnp.concatenate([attn_out, conv_out], axis=-1) @ w_o[l]
        x = x + gt_m * rms_norm(mix, g_post_mix[l])

        h = rms_norm(x, g_pre_ffn[l]) * (1.0 + sc_f) + sh_f
        u = causal_dwconv3(h @ w_up[l], conv_w_ffn[l], conv_b_ffn[l])
        a, g = jnp.split(u, 2, axis=-1)
        y = (jax.nn.silu(g) * a) @ w_down[l]
        x = x + gt_f * rms_norm(y, g_post_ffn[l])
    return x
```

```python
import numpy as np
import ml_dtypes
import concourse.bass as bass
import concourse.mybir as mybir
from concourse.bass_utils import run_bass_kernel_spmd

F32 = mybir.dt.float32
BF16 = mybir.dt.bfloat16
I32 = mybir.dt.int32
U8 = mybir.dt.uint8
ALU = mybir.AluOpType
AF = mybir.ActivationFunctionType
AX = mybir.AxisListType
ENGS = ("pe", "act", "dve", "pool", "sp")

D = 2048
S = 8192
HO = 2064
DFF = 5632
NEG = -30000.0
PI = float(np.pi)
TWO_PI = float(2 * np.pi)
ARENA = 204 * 1024


class Op:
    __slots__ = ("eng", "fn", "deps", "inc", "semval", "dma", "chan", "key", "raw")

    def __init__(self):
        self.inc = False
        self.semval = None
        self.dma = False
        self.chan = None


def _base(t):
    return t if isinstance(t, str) else t[0]


class Prog:
    def __init__(self, nc):
        self.nc = nc
        self.ops = []
        self.lastw = {}
        self.lastr = {}
        self.chan_cnt = {}
        self.uid = 0
        self.bytok = {}
        self.alias_deps = {}
        self.split = {}

    def alias(self, new, old):
        ops = {}
        for t in self.bytok.get(old, ()):
            w = self.lastw.get(t)
            if w is not None:
                ops[id(w)] = w
            for r in self.lastr.get(t, {}).values():
                ops[id(r)] = r
        if ops:
            self.alias_deps.setdefault(new, {}).update(ops)

    def _add(self, op, reads, writes):
        deps = {}
        raw = set()
        for t in reads:
            w = self.lastw.get(t)
            if w is not None:
                deps[id(w)] = w
                raw.add(id(w))
        for t in writes:
            w = self.lastw.get(t)
            if w is not None:
                deps[id(w)] = w
            for r in self.lastr.get(t, {}).values():
                deps[id(r)] = r
        for t in list(reads) + list(writes):
            b = _base(t)
            self.bytok.setdefault(b, set()).add(t)
            ad = self.alias_deps.get(b)
            if ad:
                deps.update(ad)
        deps.pop(id(op), None)
        op.deps = list(deps.values())
        op.raw = raw
        for t in reads:
            self.lastr.setdefault(t, {})[op.key] = op
        for t in writes:
            self.lastw[t] = op
            self.lastr[t] = {}
        self.ops.append(op)
        return op

    def op(self, eng, fn, reads=(), writes=()):
        o = Op()
        o.eng = eng
        o.fn = fn
        o.key = eng
        return self._add(o, reads, writes)

    def dma(self, queue, chan, fn, reads=(), writes=()):
        o = Op()
        o.eng = queue
        o.fn = fn
        o.dma = True
        o.chan = chan
        self.uid += 1
        o.key = ("dma", self.uid)
        self.chan_cnt[chan] = self.chan_cnt.get(chan, 0) + 1
        o.semval = 16 * self.chan_cnt[chan]
        return self._add(o, reads, writes)

    def emit(self, final_chans=()):
        nc = self.nc
        engobj = {"pe": nc.tensor, "act": nc.scalar, "dve": nc.vector, "pool": nc.gpsimd, "sp": nc.sync}

        def skip(op, d):
            return (not op.dma) and (not d.dma) and d.eng == op.eng and (op.eng == "pe" or id(d) not in op.raw)

        for op in self.ops:
            for d in op.deps:
                if d.dma or skip(op, d):
                    continue
                d.inc = True
        sems = {e: nc.alloc_semaphore("s_" + e) for e in ENGS}
        csems = {c: nc.alloc_semaphore("c_%d" % i) for i, c in enumerate(self.chan_cnt)}
        counts = {e: 0 for e in ENGS}
        waited = {}
        for op in self.ops:
            e = engobj[op.eng]
            need = {}
            for d in op.deps:
                if d.dma:
                    k = ("c", d.chan)
                    v = 16 * self.chan_cnt[d.chan] if d.chan in ("const", "pc") else d.semval
                elif skip(op, d):
                    continue
                else:
                    k = ("e", d.eng)
                    v = d.semval
                if need.get(k, 0) < v:
                    need[k] = v
            for k, v in need.items():
                wk = (op.eng, k)
                if waited.get(wk, 0) < v:
                    e.wait_ge(csems[k[1]] if k[0] == "c" else sems[k[1]], v)
                    waited[wk] = v
            if op.dma:
                try:
                    n0 = nc.n_instructions
                    n0 = n0() if callable(n0) else n0
                except Exception:
                    n0 = None
            ins = op.fn(e)
            if op.dma:
                if n0 is not None:
                    n1 = nc.n_instructions
                    n1 = n1() if callable(n1) else n1
                    if n1 - n0 != 1:
                        self.split.setdefault(str(op.chan), []).append(n1 - n0)
                ins.then_inc(csems[op.chan], 16)
            elif op.inc:
                counts[op.eng] += 1
                op.semval = counts[op.eng]
                ins.then_inc(sems[op.eng], 1)
        for c in final_chans:
            nc.sync.wait_ge(csems[c], 16 * self.chan_cnt[c])
        return counts


class Arena:
    def __init__(self, nc, P):
        self.ap = nc.alloc_sbuf_tensor("arena", [128, ARENA], U8).ap()
        self.top = 0
        self.hist = []
        self.P = P
        self.peak = 0

    def alloc(self, name, shape, dtype, parts=128):
        n = int(np.prod(shape)) * mybir.dt.size(dtype)
        start = (self.top + 63) // 64 * 64
        end = start + n
        assert end <= ARENA, (name, end)
        self.top = end
        self.peak = max(self.peak, end)
        for (nm, s, e) in self.hist:
            if s < end and e > start and nm != name:
                self.P.alias(name, nm)
        self.hist.append((name, start, end))
        v = self.ap[0:parts, start:end].bitcast(dtype)
        if len(shape) == 2:
            v = v.rearrange("p (a b) -> p a b", a=shape[0])
        elif len(shape) == 3:
            v = v.rearrange("p (a b c) -> p a b c", a=shape[0], b=shape[1])
        return v

    def mark(self):
        return self.top

    def release(self, m):
        self.top = m


class Rot:
    def __init__(self, items):
        self.items = list(items)
        self.i = 0

    def next(self):
        v = self.items[self.i % len(self.items)]
        self.i += 1
        return v


def build(stop_after=9):
    nc = bass.Bass("TRN2", target_bir_lowering=False)

    def din(name, shape, dt=F32):
        return nc.dram_tensor(name, list(shape), dt, kind="ExternalInput").ap()

    x_all = din("x_all", [S, D])
    x_own = din("x_own", [HO, D])
    pos_all = din("pos_all", [64, S], I32)
    pos_own = din("pos_own", [64, HO], I32)
    c_t = din("c_t", [128, 16])
    w_ada = din("w_ada", [24, 128, 16, 512])
    b_ada = din("b_ada", [1, 6 * D])
    g_pre_mix = din("g_pre_mix", [128, D])
    g_post_mix = din("g_post_mix", [128, D])
    g_pre_ffn = din("g_pre_ffn", [128, D])
    g_post_ffn = din("g_post_ffn", [128, 16])
    w_kv = din("w_kv", [128, 16, 640])
    w_q = din("w_q", [128, 16, 768])
    w_conv = din("w_conv", [24, 128, 16, 128])
    w_uq = din("w_uq", [8, 128, 6, 256])
    w_uk = din("w_uk", [128, 4, 1024])
    w_uv = din("w_uv", [128, 4, 1024])
    w_o = din("w_o", [128, 16, D])
    w_up = din("w_up", [88, 128, 16, 128])
    w_down = din("w_down", [16, 128, 44, 128])
    g_q = din("g_q", [128, 6])
    g_kv = din("g_kv", [128, 4])
    cw_mix = din("cw_mix", [128, 8, 3])
    cb_mix = din("cb_mix", [128, 8])
    cw_a = din("cw_a", [128, 44, 3])
    cw_g = din("cw_g", [128, 44, 3])
    cb_a = din("cb_a", [128, 44])
    cb_g = din("cb_g", [128, 44])
    invf_d = din("invf", [64, 1])
    sgn_d = din("sgn", [64, 1])
    hval_d = din("hval", [128, 16])
    mask_own_d = din("mask_own", [128, 16, 512], BF16)
    mask_halo_d = din("mask_halo", [128, 64, 16], BF16)
    out = nc.dram_tensor("out", [2048, D], F32, kind="ExternalOutput").ap()

    def dscr(name, shape, dt):
        return nc.dram_tensor(name, list(shape), dt, kind="Internal").ap()

    mod_d = dscr("mod_d", [128, 6 * D], F32)
    kT_d = dscr("kT_d", [8, 128, S], BF16)
    v_d = dscr("v_d", [S, 1024], BF16)
    kr_d = dscr("kr_d", [64, S], BF16)
    co_d = dscr("co_d", [8, 128, HO], BF16)
    ao_d = dscr("ao_d", [8, 128, HO], BF16)
    x1_d = dscr("x1_d", [HO, D], F32)

    P = Prog(nc)
    A = Arena(nc, P)
    banks = [nc.alloc_psum_tensor("b%d" % i, [128, 512], F32).ap() for i in range(8)]

    def BK(i):
        return ("bank", i)

    def mm(bi, out_ap, lhsT, rhs, start, stop, reads):
        P.op("pe", lambda e: e.matmul(out_ap, lhsT=lhsT, rhs=rhs, start=start, stop=stop), reads=reads, writes=[BK(bi)])

    out_chans = []

    def load(queue, chan, dst, src, tok, reads=(), slow=False):
        w = [tok]
        if chan == "const":
            ch = "const"
        elif chan in ("bc", "mcl", "wo", "mk"):
            ch = chan
            w.append(("ord", chan))
        else:
            ch = ("L", tok)
        if slow:
            P.dma(queue, ch, lambda e: e.dma_start(out=dst, in_=src, allow_slow_non_contiguous=True), reads=reads, writes=w)
        else:
            P.dma(queue, ch, lambda e: e.dma_start(out=dst, in_=src), reads=reads, writes=w)

    def store(queue, chan, dst, src, rtok, wtok=None):
        ch = ("S", rtok)
        if chan == "outw" and ch not in out_chans:
            out_chans.append(ch)
        P.dma(queue, ch, lambda e: e.dma_start(out=dst, in_=src), reads=[rtok], writes=([wtok] if wtok else []))

    ident = A.alloc("ident", [128], BF16)
    identf = A.alloc("identf", [128], F32)
    ones = A.alloc("ones", [128], BF16)
    onesf = A.alloc("onesf", [128], F32)
    eps = A.alloc("eps", [1], F32)
    gkv = A.alloc("gkv", [4], F32)
    gq = A.alloc("gq", [6], F32)
    cwm = A.alloc("cwm", [8, 3], F32)
    cbm = A.alloc("cbm", [8], F32)
    cwa = A.alloc("cwa", [44, 3], F32)
    cwg = A.alloc("cwg", [44, 3], F32)
    cba = A.alloc("cba", [44], F32)
    cbg = A.alloc("cbg", [44], F32)
    gpf = A.alloc("gpf", [16], F32)
    g2c = A.alloc("g2c", [16], F32)
    invf = A.alloc("invf", [1], F32, parts=64)
    sgn = A.alloc("sgn", [1], F32, parts=64)
    hval = A.alloc("hval", [4, 4], F32)
    junk = A.alloc("junk", [2048], BF16)
    st = A.alloc("st", [64], F32)

    P.op("pool", lambda e: e.memset(identf, 0.0), writes=["identf"])
    P.op("pool", lambda e: e.affine_select(out=identf, in_=identf, pattern=[[-1, 128]], compare_op=ALU.not_equal,
                                            fill=1.0, base=0, channel_multiplier=1), reads=["identf"], writes=["identf"])
    P.op("dve", lambda e: e.tensor_copy(out=ident, in_=identf), reads=["identf"], writes=["ident"])
    P.op("dve", lambda e: e.memset(ones, 1.0), writes=["ones"])
    P.op("dve", lambda e: e.memset(onesf, 1.0), writes=["onesf"])
    P.op("dve", lambda e: e.memset(eps, 1e-6), writes=["eps"])
    for dst, src, nm in [(gkv, g_kv, "gkv"), (gq, g_q, "gq"), (cwm, cw_mix, "cwm"), (cbm, cb_mix, "cbm"), (cwa, cw_a, "cwa"),
                         (cwg, cw_g, "cwg"), (cba, cb_a, "cba"), (cbg, cb_g, "cbg"), (gpf, g_post_ffn, "gpf"),
                         (invf, invf_d, "invf"), (sgn, sgn_d, "sgn"),
                         (hval, hval_d.rearrange("p (a b) -> p a b", a=4), "hval")]:
        load("sp", "const", dst, src, nm)

    def rstd_from(dst, src, n_inv, parts=128, reads=(), wtok=None, xw=()):
        w = [wtok] if wtok else []
        P.op("act", lambda e: e.activation(out=dst, in_=src, func=AF.Sqrt, scale=n_inv, bias=eps[0:parts, 0:1]),
             reads=list(reads) + ["eps"], writes=w + list(xw))
        P.op("dve", lambda e: e.reciprocal(out=dst, in_=dst), reads=w, writes=w)

    m0 = A.mark()
    cT = A.alloc("cT", [16], F32)
    wad = [A.alloc("wad", [16, 512], F32) for s in range(2)]
    modrow = [A.alloc("modrow", [512], F32, parts=1) for s in range(2)]
    bad = [A.alloc("bad", [512], F32, parts=1) for s in range(2)]
    load("sp", "const", cT, c_t, "cT")
    P.op("act", lambda e: e.activation(out=cT, in_=cT, func=AF.Silu), reads=["cT"], writes=["cT"])
    modt = [A.alloc("modt", [512], F32) for s in range(2)]
    def p0_tail(n):
        s = n % 2
        mm(2 + s, banks[2 + s], onesf[0:1, :], modrow[s], True, True, ["onesf", ("modrow", s)])
        P.op("act", lambda e, s=s: e.copy(out=modt[s], in_=banks[2 + s]), writes=[("modt", s), BK(2 + s)])
        store("pool", "modw", mod_d[:, n * 512:(n + 1) * 512], modt[s], ("modt", s), "mod_d")

    for n in range(24):
        s = n % 2
        load("sp", ("wad", s), wad[s], w_ada[n], ("wad", s))
        load("sp", ("bad", s), bad[s], b_ada[0:1, n * 512:(n + 1) * 512], ("bad", s))
        for k in range(16):
            mm(s, banks[s][0:1, :], cT[:, k:k + 1], wad[s][:, k, :], k == 0, k == 15, ["cT", ("wad", s)])
        P.op("dve", lambda e, s=s: e.tensor_tensor(out=modrow[s], in0=banks[s][0:1, :], in1=bad[s], op=ALU.add),
             reads=[("bad", s)], writes=[("modrow", s), BK(s)])
        if n > 0:
            p0_tail(n - 1)
    p0_tail(23)
    A.release(m0)

    wupb = dscr("wupb", [88, 128, 2048], BF16)
    wdnb = dscr("wdnb", [16, 128, 44 * 128], BF16)
    pc_list = [(wupb[c2], w_up[c2].rearrange("p k c -> p (k c)"), ("wupb", c2)) for c2 in range(88)] + \
              [(wdnb[c2], w_down[c2].rearrange("p k c -> p (k c)"), ("wdnb", c2)) for c2 in range(16)]
    pc_pos = [0]

    def precast(n):
        for _ in range(n):
            if pc_pos[0] >= len(pc_list) or stop_after < 6:
                return
            dst, src, tok = pc_list[pc_pos[0]]
            pc_pos[0] += 1
            P.dma("pool", "pc", lambda e, dst=dst, src=src: e.dma_start(out=dst, in_=src), writes=[tok])

    def bc_load(dst, src_row, tok, reads=()):
        load("sp", "bc", dst, src_row, tok, reads)

    def mod_row(i):
        return mod_d[:, i * D:(i + 1) * D]

    def sincos(posi, N, C, Sp, tmp, tag, eng="dve", defer=False):
        ang, a2s, a2c, ki, kf = tmp
        P.op(eng, lambda e: e.tensor_copy(out=ang, in_=posi), reads=[tag + "posi"], writes=[tag + "ang"])
        P.op(eng, lambda e: e.tensor_scalar(out=ang, in0=ang, scalar1=invf[:, 0:1], scalar2=None, op0=ALU.mult),
             reads=[tag + "ang", "invf"], writes=[tag + "ang"])
        fins = []
        for shift, dst, dtok, a2, T2 in ((0.0, Sp, tag + "Sp", a2s, tag + "a2s"), (PI / 2, C, tag + "C", a2c, tag + "a2c")):
            TK = tag + "kf"
            P.op(eng, lambda e, shift=shift, a2=a2: e.tensor_scalar(out=a2, in0=ang, scalar1=1.0, scalar2=shift, op0=ALU.mult, op1=ALU.add),
                 reads=[tag + "ang"], writes=[T2])
            P.op(eng, lambda e, a2=a2: e.tensor_scalar(out=ki, in0=a2, scalar1=1.0 / TWO_PI, scalar2=None, op0=ALU.mult),
                 reads=[T2], writes=[tag + "ki"])
            P.op(eng, lambda e: e.tensor_copy(out=kf, in_=ki), reads=[tag + "ki"], writes=[TK])
            P.op(eng, lambda e: e.tensor_scalar(out=kf, in0=kf, scalar1=-TWO_PI, scalar2=None, op0=ALU.mult), reads=[TK], writes=[TK])
            P.op(eng, lambda e, a2=a2: e.tensor_tensor(out=a2, in0=a2, in1=kf, op=ALU.add), reads=[TK, T2], writes=[T2])
            P.op(eng, lambda e, a2=a2: e.tensor_scalar(out=kf, in0=a2, scalar1=PI, scalar2=-TWO_PI, op0=ALU.is_gt, op1=ALU.mult),
                 reads=[T2], writes=[TK])
            P.op(eng, lambda e, a2=a2: e.tensor_tensor(out=a2, in0=a2, in1=kf, op=ALU.add), reads=[TK, T2], writes=[T2])
            P.op(eng, lambda e, a2=a2: e.tensor_scalar(out=kf, in0=a2, scalar1=-PI, scalar2=TWO_PI, op0=ALU.is_lt, op1=ALU.mult),
                 reads=[T2], writes=[TK])
            P.op(eng, lambda e, a2=a2: e.tensor_tensor(out=a2, in0=a2, in1=kf, op=ALU.add), reads=[TK, T2], writes=[T2])
            P.op(eng, lambda e, a2=a2: e.tensor_scalar(out=a2, in0=a2, scalar1=3.1415925, scalar2=-3.1415925, op0=ALU.min, op1=ALU.max),
                 reads=[T2], writes=[T2])
            fins.append((dst, dtok, a2, T2))

        def finish():
            for dst, dtok, a2, T2 in fins:
                P.op("act", lambda e, dst=dst, a2=a2: e.activation(out=dst, in_=a2, func=AF.Sin), reads=[T2], writes=[dtok])
            P.op(eng, lambda e: e.tensor_scalar(out=Sp, in0=Sp, scalar1=sgn[:, 0:1], scalar2=None, op0=ALU.mult),
                 reads=[tag + "Sp", "sgn"], writes=[tag + "Sp"])

        if defer:
            return finish
        finish()
        return None

    evac_rr = Rot(["act", "dve"])

    def evac_copy(dst, src, bi, wtok, eng=None, reads=()):
        eng = eng or evac_rr.next()
        if eng == "act":
            P.op("act", lambda e: e.copy(out=dst, in_=src), reads=reads, writes=[wtok, BK(bi)])
        else:
            P.op("dve", lambda e: e.tensor_copy(out=dst, in_=src), reads=reads, writes=[wtok, BK(bi)])

    tr_banks = Rot([0, 1])

    class NT:
        def __init__(self, Abc, Bbc, tag, depth=3):
            self.Abc, self.Bbc, self.tag = Abc, Bbc, tag
            self.depth = depth
            self.xt = [A.alloc(tag + "xt", [D], F32) for s in range(depth)]
            self.hb = [A.alloc(tag + "hb", [D], BF16) for s in range(depth)]
            self.i = 0
            self.pending = []

        def tile(self, rows_ap, n, dst, dtok, col0):
            tag = self.tag
            s = self.i % self.depth
            self.i += 1
            xt, hb = self.xt[s], self.hb[s]
            Abc, Bbc = self.Abc, self.Bbc
            XT, HB, SQ, RS = (tag + "xt", s), (tag + "hb", s), ("st", s), ("st", 3 + s)
            load("sp", XT, xt[0:n], rows_ap, XT)
            P.op("act", lambda e: e.activation(out=junk[0:n], in_=xt[0:n], func=AF.Square, accum_out=st[0:n, s:s + 1]),
                 reads=[XT], writes=["junk", SQ])
            rstd_from(st[0:n, 3 + s:4 + s], st[0:n, s:s + 1], 1.0 / D, parts=n, reads=[SQ], wtok=RS)
            P.op("dve", lambda e: e.scalar_tensor_tensor(out=xt[0:n], in0=xt[0:n], scalar=st[0:n, 3 + s:4 + s], in1=Abc[0:n],
                                                           op0=ALU.mult, op1=ALU.mult), reads=[XT, RS, tag + "A"], writes=[XT])
            P.op("pool", lambda e: e.tensor_tensor(out=hb[0:n], in0=xt[0:n], in1=Bbc[0:n], op=ALU.add),
                 reads=[XT, tag + "B"], writes=[HB])
            self.pending.append((hb, HB, n, dst, dtok, col0))
            if len(self.pending) >= self.depth:
                self.stage_b()

        def flush(self):
            while self.pending:
                self.stage_b()

        def stage_b(self):
            hb, HB, n, dst, dtok, col0 = self.pending.pop(0)
            for kg in range(4):
                bi = tr_banks.next()
                bb = banks[bi].bitcast(BF16)
                for jj in range(4):
                    kc = kg * 4 + jj
                    P.op("pe", lambda e, jj=jj, kc=kc, bb=bb: e.transpose(out=bb[:, jj * 128:jj * 128 + n], in_=hb[0:n, kc * 128:(kc + 1) * 128],
                                                                            identity=ident[0:n, 0:n]),
                         reads=[HB, "ident"], writes=[BK(bi)])
                src = bb[:, 0:512].rearrange("p (j c) -> p j c", j=4)[:, :, 0:n]
                evac_copy(dst[:, kg * 4:(kg + 1) * 4, col0:col0 + n], src, bi, dtok)

    m1 = A.mark()
    A1 = A.alloc("p1A", [D], F32)
    B1 = A.alloc("p1B", [D], F32)
    tmpb = A.alloc("p1tmpb", [D], F32)
    bc_load(A1, g_pre_mix, "p1A")
    bc_load(tmpb, mod_row(1), "p1tmpb", reads=["mod_d"])
    P.op("dve", lambda e: e.scalar_tensor_tensor(out=A1, in0=tmpb, scalar=1.0, in1=A1, op0=ALU.add, op1=ALU.mult),
         reads=["p1A", "p1tmpb"], writes=["p1A"])
    bc_load(B1, mod_row(0), "p1B", reads=["mod_d"])
    m1b = A.mark()
    nt1 = NT(A1, B1, "p1")
    hT = [A.alloc("hT", [16, 512], BF16) for s in range(2)]
    wkv = A.alloc("wkv", [16, 640], BF16)
    wuk = A.alloc("wuk", [4, 1024], BF16)
    wuv = A.alloc("wuv", [4, 1024], BF16)
    kvl = A.alloc("kvl", [4, 512], F32)
    sq = [A.alloc("sq", [512], BF16) for s in range(4)]
    rbc = A.alloc("rbc", [512], F32)
    kvn = A.alloc("kvn", [4, 512], BF16)
    kst = A.alloc("kst", [8, 512], BF16)
    vst = A.alloc("vst", [4, 1024], BF16)
    posi1 = A.alloc("p1posi", [512], I32, parts=64)
    rtmp = [A.alloc("p1" + nm, [512], (I32 if nm == "ki" else F32), parts=64) for nm in ("ang", "a2s", "a2c", "ki", "kf")]
    C1 = A.alloc("p1C", [512], F32, parts=64)
    S1 = A.alloc("p1Sp", [512], F32, parts=64)
    t1 = A.alloc("p1t1", [512], F32, parts=64)
    t2 = A.alloc("p1t2", [512], F32, parts=64)
    krst = A.alloc("krst", [512], BF16, parts=64)
    load("pool", "wkv", wkv, w_kv, "wkv")
    load("pool", "wuk", wuk, w_uk, "wuk")
    load("pool", "wuv", wuv, w_uv, "wuv")
    mmb = Rot([2, 3, 4, 5])
    NG1 = 16 if stop_after >= 1 else 0
    p1_tiles = [(g, i) for g in range(NG1) for i in range(4)]
    p1_next = [0]

    def p1_submit(k=1):
        for _ in range(k):
            if p1_next[0] < len(p1_tiles):
                g2, i2 = p1_tiles[p1_next[0]]
                p1_next[0] += 1
                r0 = g2 * 512 + i2 * 128
                nt1.tile(x_all[r0:r0 + 128, :], 128, hT[g2 % 2], ("hT", g2 % 2), i2 * 128)
            else:
                nt1.flush()

    p1_submit(6)

    p1_fin = [None]

    def p1_x(g):
        hs = g % 2
        HT = ("hT", hs)
        if g == NG1 - 1:
            nt1.flush()
        for cc in range(4):
            bi = mmb.next()
            for k in range(16):
                mm(bi, banks[bi], wkv[:, k, cc * 128:(cc + 1) * 128], hT[hs][:, k, :], k == 0, k == 15, ["wkv", HT])
            P.op("act", lambda e, bi=bi, cc=cc: e.copy(out=kvl[:, cc, :], in_=banks[bi]), writes=[("kvl", cc), BK(bi)])
            P.op("dve", lambda e, cc=cc: e.tensor_tensor(out=sq[cc], in0=kvl[:, cc, :], in1=kvl[:, cc, :], op=ALU.mult),
                 reads=[("kvl", cc)], writes=[("sq", cc)])
            if cc == 1:
                p1_submit()
        if p1_fin[0] is not None:
            p1_fin[0]()
            p1_fin[0] = None
        for which, c0, bi in ((0, 512, 7), (1, 576, mmb.next())):
            for k in range(16):
                mm(bi, banks[bi][0:64, :], wkv[:, k, c0:c0 + 64], hT[hs][:, k, :], k == 0, k == 15, ["wkv", HT])
            tb = t1 if which == 0 else t2
            tt = C1 if which == 0 else S1
            P.op("dve", lambda e, bi=bi, tb=tb, tt=tt: e.tensor_tensor(out=tb, in0=banks[bi][0:64, :], in1=tt, op=ALU.mult),
                 reads=["p1C" if which == 0 else "p1Sp"], writes=["p1t1" if which == 0 else "p1t2", BK(bi)])
        for cc in range(4):
            mm(6, banks[6], ones, sq[cc], cc == 0, cc == 3, ["ones", ("sq", cc)])
        P.op("pool", lambda e: e.tensor_tensor(out=krst, in0=t1, in1=t2, op=ALU.add), reads=["p1t1", "p1t2"], writes=["krst"])
        store("pool", "krw", kr_d[:, g * 512:(g + 1) * 512], krst, "krst", "kr_d")
        p1_submit()

    def p1_sincos(g):
        if g < NG1:
            load("sp", "p1pos", posi1, pos_all[:, g * 512:(g + 1) * 512], "p1posi")
            p1_fin[0] = sincos(posi1, 512, C1, S1, rtmp, "p1", eng="dve", defer=True)

    def p1_xtail(g):
        rstd_from(rbc, banks[6], 1.0 / 512, wtok="rbc", xw=[BK(6)])
        for cc in range(4):
            P.op("dve", lambda e, cc=cc: e.scalar_tensor_tensor(out=kvn[:, cc, :], in0=kvl[:, cc, :], scalar=gkv[:, cc:cc + 1], in1=rbc,
                                                                 op0=ALU.mult, op1=ALU.mult),
                 reads=[("kvl", cc), "gkv", "rbc"], writes=["kvn"])

    def p1_y(g):
        for h in range(8):
            bi = mmb.next()
            for k in range(4):
                mm(bi, banks[bi], wuk[:, k, h * 128:(h + 1) * 128], kvn[:, k, :], k == 0, k == 3, ["wuk", "kvn"])
            evac_copy(kst[:, h, :], banks[bi], bi, "kst", eng="act")
        store("pool", "kw", kT_d[:, :, g * 512:(g + 1) * 512].rearrange("h p t -> p h t"), kst, "kst", "kT_d")
        p1_submit()
        for i in range(4):
            for hh in range(2):
                bi = mmb.next()
                for k in range(4):
                    mm(bi, banks[bi], kvn[:, k, i * 128:(i + 1) * 128], wuv[:, k, hh * 512:(hh + 1) * 512], k == 0, k == 3, ["wuv", "kvn"])
                evac_copy(vst[:, i, hh * 512:(hh + 1) * 512], banks[bi], bi, "vst", eng="act")
        store("pool", "vw", v_d[g * 512:(g + 1) * 512, :].rearrange("(i p) c -> p i c", p=128), vst, "vst", "v_d")
        precast(3)
        p1_submit()

    p1_sincos(0)
    for g in range(NG1):
        p1_x(g)
        if g > 0:
            p1_y(g - 1)
        else:
            p1_submit(2)
        p1_xtail(g)
        p1_sincos(g + 1)
    if NG1:
        p1_y(NG1 - 1)
    A.release(m1)

    qg = A.alloc("qg", [6, HO], BF16)
    rsq = A.alloc("rsq", [HO], F32)
    mQ = A.mark()
    hTo = A.alloc("hTo", [16, HO], BF16)
    m2 = A.mark()
    A1b = A.alloc("p2A", [D], F32)
    B1b = A.alloc("p2B", [D], F32)
    tmpb2 = A.alloc("p2tmpb", [D], F32)
    bc_load(A1b, g_pre_mix, "p2A")
    bc_load(tmpb2, mod_row(1), "p2tmpb", reads=["mod_d"])
    P.op("dve", lambda e: e.scalar_tensor_tensor(out=A1b, in0=tmpb2, scalar=1.0, in1=A1b, op0=ALU.add, op1=ALU.mult),
         reads=["p2A", "p2tmpb"], writes=["p2A"])
    bc_load(B1b, mod_row(0), "p2B", reads=["mod_d"])
    nt2 = NT(A1b, B1b, "p2")
    wq = A.alloc("wq", [16, 768], BF16)
    load("pool", "wq", wq, w_q, "wq")
    sq2 = [A.alloc("p2sq", [512], BF16) for s in range(2)]
    groups = [(0, 16)] + [(16 + 512 * m, 512) for m in range(4)]
    if stop_after >= 2:
        nt2.tile(x_own[0:16, :], 16, hTo, "hTo", 0)
        for t in range(16):
            nt2.tile(x_own[16 + t * 128:16 + (t + 1) * 128, :], 128, hTo, "hTo", 16 + t * 128)
        nt2.flush()
        precast(8)
        for (c0, N) in groups:
            for cc in range(6):
                bi = mmb.next()
                for k in range(16):
                    mm(bi, banks[bi][:, 0:N], wq[:, k, cc * 128:(cc + 1) * 128], hTo[:, k, c0:c0 + N], k == 0, k == 15, ["wq", "hTo"])
                P.op("act", lambda e, bi=bi, cc=cc, c0=c0, N=N: e.activation(out=qg[:, cc, c0:c0 + N], in_=banks[bi][:, 0:N], func=AF.Identity,
                                                                               scale=gq[:, cc:cc + 1]),
                     reads=["gq"], writes=["qg", BK(bi)])
                P.op("act", lambda e, bi=bi, cc=cc, N=N: e.activation(out=sq2[cc % 2][:, 0:N], in_=banks[bi][:, 0:N], func=AF.Square),
                     writes=[("p2sq", cc % 2), BK(bi)])
                mm(6, banks[6][:, 0:N], ones, sq2[cc % 2][:, 0:N], cc == 0, cc == 5, ["ones", ("p2sq", cc % 2)])
            rstd_from(rsq[:, c0:c0 + N], banks[6][:, 0:N], 1.0 / 768, wtok="rsq", xw=[BK(6)])
    A.release(m2)

    m2b = A.mark()
    wcr = [A.alloc("wcr", [16, 128], BF16) for s in range(6)]
    gcs = A.alloc("gcs", [4, 516], F32)
    gbs = A.alloc("gbs", [4, 516], F32)
    mmx = A.alloc("mmx", [4, 516], F32)
    yy = A.alloc("yy", [4, 516], F32)
    cst = [A.alloc("cst", [HO], BF16) for s in range(2)]
    P.op("dve", lambda e: e.memset(yy, 0.0), writes=["yy"])

    def eb_dst(buf, c0, N):
        if N == 16:
            return buf[:, :, 0:4]
        m = (c0 - 16) // 512
        return buf[:, m, 4:516]

    def eb_src(bi, N):
        if N == 16:
            return banks[bi][:, 0:16].rearrange("p (m i) -> p m i", m=4)
        return banks[bi][:, 0:512]

    NI = 8 if stop_after >= 3 else 0
    wci = 0
    for i in range(NI):
        slots = []
        for j in range(3):
            s = wci % 6
            wci += 1
            load("pool", ("wcr", s), wcr[s], w_conv[i * 3 + j], ("wcr", s))
            slots.append(s)
        for j, kind in ((1, "gc"), (2, "ci"), (0, "gb")):
            s = slots[j]
            for (c0, N) in groups:
                bi = mmb.next()
                for k in range(16):
                    mm(bi, banks[bi][:, 0:N], wcr[s][:, k, :], hTo[:, k, c0:c0 + N], k == 0, k == 15, [("wcr", s), "hTo"])
                if kind == "gc":
                    P.op("act", lambda e, bi=bi, c0=c0, N=N: e.copy(out=eb_dst(gcs, c0, N), in_=eb_src(bi, N)), writes=["gcs", BK(bi)])
                elif kind == "ci":
                    P.op("dve", lambda e, bi=bi, c0=c0, N=N: e.tensor_tensor(out=eb_dst(mmx, c0, N), in0=eb_src(bi, N), in1=eb_dst(gcs, c0, N),
                                                                              op=ALU.mult), reads=["gcs"], writes=["mmx", BK(bi)])
                else:
                    P.op("act", lambda e, bi=bi, c0=c0, N=N: e.copy(out=eb_dst(gbs, c0, N), in_=eb_src(bi, N)), writes=["gbs", BK(bi)])
            if kind == "ci":
                P.op("dve", lambda e: e.tensor_tensor(out=mmx[:, :, 0:4], in0=mmx[:, :, 0:4], in1=hval, op=ALU.mult),
                     reads=["mmx", "hval"], writes=["mmx"])
                P.op("dve", lambda e, i=i: e.tensor_scalar(out=yy[:, :, 2:516], in0=mmx[:, :, 2:516], scalar1=cwm[:, i, 2:3], scalar2=cbm[:, i:i + 1],
                                                            op0=ALU.mult, op1=ALU.add), reads=["mmx", "cwm", "cbm"], writes=["yy"])
                P.op("dve", lambda e, i=i: e.scalar_tensor_tensor(out=yy[:, :, 2:516], in0=mmx[:, :, 1:515], scalar=cwm[:, i, 1:2], in1=yy[:, :, 2:516],
                                                                    op0=ALU.mult, op1=ALU.add), reads=["mmx", "cwm", "yy"], writes=["yy"])
                P.op("dve", lambda e, i=i: e.scalar_tensor_tensor(out=yy[:, :, 2:516], in0=mmx[:, :, 0:514], scalar=cwm[:, i, 0:1], in1=yy[:, :, 2:516],
                                                                   op0=ALU.mult, op1=ALU.add), reads=["mmx", "cwm", "yy"], writes=["yy"])
        cs = i % 2
        P.op("pool", lambda e, cs=cs: e.tensor_tensor(out=cst[cs][:, 0:16].rearrange("p (m i) -> p m i", m=4), in0=gbs[:, :, 0:4], in1=yy[:, :, 0:4],
                                                       op=ALU.mult), reads=["gbs", "yy"], writes=[("cst", cs)])
        P.op("dve", lambda e, cs=cs: e.tensor_tensor(out=cst[cs][:, 16:HO].rearrange("p (m i) -> p m i", m=4), in0=gbs[:, :, 4:516], in1=yy[:, :, 4:516],
                                                      op=ALU.mult), reads=["gbs", "yy"], writes=[("cst", cs)])
        store("pool", "cow", co_d[i], cst[cs], ("cst", cs), "co_d")
        precast(2)
    A.release(mQ)

    m3 = A.mark()
    Cq = A.alloc("p3C", [HO], F32, parts=64)
    Sq = A.alloc("p3Sp", [HO], F32, parts=64)
    m3t = A.mark()
    posi3 = A.alloc("p3posi", [HO], I32, parts=64)
    rtmp3 = [A.alloc("p3" + nm, [HO], (I32 if nm == "ki" else F32), parts=64) for nm in ("ang", "a2s", "a2c", "ki", "kf")]
    NH = 8 if stop_after >= 4 else 0
    if NH:
        load("sp", "p3pos", posi3, pos_own, "p3posi")
        sincos(posi3, HO, Cq, Sq, rtmp3, "p3")
        P.op("dve", lambda e: e.tensor_tensor(out=Cq, in0=Cq, in1=rsq[0:64, :], op=ALU.mult), reads=["p3C", "rsq"], writes=["p3C"])
        P.op("dve", lambda e: e.tensor_tensor(out=Sq, in0=Sq, in1=rsq[0:64, :], op=ALU.mult), reads=["p3Sp", "rsq"], writes=["p3Sp"])
    A.release(m3t)
    Kh = [A.alloc("Kh", [S], BF16) for s in range(2)]
    Vh = [A.alloc("Vh", [64, 128], BF16) for s in range(2)]
    krT = A.alloc("krT", [S], BF16, parts=64)
    wuq = [A.alloc("wuq", [6, 256], BF16) for s in range(2)]
    qn = [A.alloc("qn", [HO], BF16) for s in range(2)]
    qr = [A.alloc("qr", [HO], BF16, parts=64) for s in range(2)]
    qt1 = A.alloc("qt1", [512], F32, parts=64)
    qt2 = A.alloc("qt2", [512], F32, parts=64)
    mko = A.alloc("mko", [16, 512], BF16)
    mkh = A.alloc("mkh", [64, 16], BF16)
    pT = [A.alloc("pT", [512], BF16) for s in range(4)]
    rec = A.alloc("rec", [512], F32)
    dacc = [A.alloc("dacc", [512], F32) for s in range(2)]
    aost = [A.alloc("aost", [HO], BF16) for s in range(1)] * 2
    SCALE = 1.0 / float(np.sqrt(192.0))
    if NH:
        load("sp", "krl", krT, kr_d, "krT", reads=["kr_d"])
        load("sp", "mk", mko, mask_own_d, "mko")
        load("sp", "mk", mkh, mask_halo_d, "mkh")

    def load_head(h):
        s = h % 2
        load("sp", ("Kh", s), Kh[s], kT_d[h], ("Kh", s), reads=["kT_d"])
        for part in range(4):
            load("sp", ("Vh", s), Vh[s][:, part * 16:(part + 1) * 16, :],
                 v_d[part * 2048:(part + 1) * 2048, h * 128:(h + 1) * 128].rearrange("(t p) d -> p t d", p=128),
                 ("Vh", s, part), reads=["v_d"])
        load("pool", ("wuq", s), wuq[s], w_uq[h], ("wuq", s))

    s_banks = Rot([0, 1, 2])
    od_banks = Rot([(3, 4, 0), (5, 6, 1)])
    if NH:
        load_head(0)
    def qproj_pieces(h):
        s = h % 2
        pieces = []
        for (c0, N) in groups:
            def nope(s=s, c0=c0, N=N):
                for k in range(6):
                    mm(7, banks[7][:, 0:N], wuq[s][:, k, 0:128], qg[:, k, c0:c0 + N], k == 0, k == 5, [("wuq", s), "qg"])
                P.op("dve", lambda e: e.tensor_tensor(out=qn[s][:, c0:c0 + N], in0=banks[7][:, 0:N], in1=rsq[:, c0:c0 + N], op=ALU.mult),
                     reads=["rsq"], writes=[("qn", s), BK(7)])

            def rope(which, s=s, c0=c0, N=N):
                w0 = 128 if which == 0 else 192
                for k in range(6):
                    mm(7, banks[7][0:64, 0:N], wuq[s][:, k, w0:w0 + 64], qg[:, k, c0:c0 + N], k == 0, k == 5, [("wuq", s), "qg"])
                tb = qt1 if which == 0 else qt2
                tt = Cq if which == 0 else Sq
                P.op("dve", lambda e: e.tensor_tensor(out=tb[:, 0:N], in0=banks[7][0:64, 0:N], in1=tt[:, c0:c0 + N], op=ALU.mult),
                     reads=["p3C" if which == 0 else "p3Sp"], writes=["qt1" if which == 0 else "qt2", BK(7)])
                if which == 1:
                    P.op("pool", lambda e: e.tensor_tensor(out=qr[s][:, c0:c0 + N], in0=qt1[:, 0:N], in1=qt2[:, 0:N], op=ALU.add),
                         reads=["qt1", "qt2"], writes=[("qr", s)])

            pieces.append(nope)
            pieces.append(lambda rope=rope: rope(0))
            pieces.append(lambda rope=rope: rope(1))
        return pieces

    if NH:
        for pc_ in qproj_pieces(0):
            pc_()
    for h in range(NH):
        s = h % 2
        precast(4)
        if h + 1 < NH:
            load_head(h + 1)
        nxt = qproj_pieces(h + 1) if h + 1 < NH else []
        steps = []
        for gi, (c0, N) in enumerate(groups):
            if gi == 0:
                kts = [(kt, ("h", kt)) for kt in range(64)]
            else:
                m = gi - 1
                nk = 16 * (m + 1)
                kts = [(kt, (("o", kt - 16 * m) if kt >= 16 * m else None)) for kt in range(nk)]
            for idx, (kt, mk) in enumerate(kts):
                steps.append((gi, c0, N, kt, mk, idx == 0, idx == len(kts) - 1))
        cur = {}

        def s_stage(stp):
            gi, c0, N, kt, mk, first, last = stp
            bi = s_banks.next()
            mm(bi, banks[bi][:, 0:N], Kh[s][:, kt * 128:(kt + 1) * 128], qn[s][:, c0:c0 + N], True, False, [("Kh", s), ("qn", s)])
            mm(bi, banks[bi][:, 0:N], krT[:, kt * 128:(kt + 1) * 128], qr[s][:, c0:c0 + N], False, mk is None, ["krT", ("qr", s)])
            if mk is not None:
                mt = mkh[:, mk[1], :] if mk[0] == "h" else mko[:, mk[1], :]
                mm(bi, banks[bi][:, 0:N], ident, mt, False, True, ["ident", "mkh", "mko"])
            return bi

        def pv_stage(stp, bi, pi):
            gi, c0, N, kt, mk, first, last = stp
            if first:
                cur["od"] = od_banks.next()
            bo, bd, da = cur["od"]
            P.op("act", lambda e: e.activation(out=pT[pi][:, 0:N], in_=banks[bi][:, 0:N], func=AF.Exp, scale=SCALE),
                 writes=[("pT", pi), BK(bi)])
            part = kt // 16
            mm(bo, banks[bo][:, 0:N], Vh[s][:, kt, :], pT[pi][:, 0:N], first, last, [("Vh", s, part), ("pT", pi)])
            idx = cur.get("idx", 0) if not first else 0
            cur["idx"] = idx + 1
            if idx % 2 == 0:
                if first:
                    P.op("dve", lambda e: e.tensor_copy(out=dacc[da][:, 0:N], in_=pT[pi][:, 0:N]), reads=[("pT", pi)], writes=[("dacc", da)])
                else:
                    P.op("dve", lambda e: e.tensor_tensor(out=dacc[da][:, 0:N], in0=dacc[da][:, 0:N], in1=pT[pi][:, 0:N], op=ALU.add),
                         reads=[("pT", pi), ("dacc", da)], writes=[("dacc", da)])
            else:
                mm(bd, banks[bd][:, 0:N], ones, pT[pi][:, 0:N], idx == 1, False, ["ones", ("pT", pi)])
            if last:
                mm(bd, banks[bd][:, 0:N], onesf, dacc[da][:, 0:N], False, True, ["onesf", ("dacc", da)])
                P.op("dve", lambda e: e.reciprocal(out=rec[:, 0:N], in_=banks[bd][:, 0:N]), writes=["rec", BK(bd)])
                P.op("dve", lambda e: e.tensor_tensor(out=aost[s][:, c0:c0 + N], in0=banks[bo][:, 0:N], in1=rec[:, 0:N], op=ALU.mult),
                     reads=["rec"], writes=[("aost", 0), BK(bo)])

        LOOK = 2
        sb = {}
        for i2 in range(len(steps) + LOOK):
            if i2 < len(steps):
                sb[i2] = s_stage(steps[i2])
            j2 = i2 - LOOK
            if j2 >= 0:
                pv_stage(steps[j2], sb.pop(j2), j2 % 4)
            if nxt and i2 % 14 == 10:
                nxt.pop(0)()
        while nxt:
            nxt.pop(0)()
        store("pool", "aow", ao_d[h], aost[s], ("aost", 0), "ao_d")
    A.release(m1)

    m4 = A.mark()
    mc = A.alloc("mc", [16, HO], BF16)
    wo = A.alloc("wo", [16, D], BF16)
    G1 = A.alloc("p4G", [D], F32)
    tg = A.alloc("p4tg", [D], F32)
    xt4 = [A.alloc("p4xt", [D], F32) for s in range(2)]
    x1t = [A.alloc("p4x1", [D], F32) for s in range(2)]
    NT4 = 17 if stop_after >= 5 else 0
    if NT4:
        load("sp", "mcl", mc[:, 0:8, :], ao_d.rearrange("h p t -> p h t"), ("mc", 0), reads=["ao_d"])
        load("sp", "mcl", mc[:, 8:16, :], co_d.rearrange("h p t -> p h t"), ("mc", 1), reads=["co_d"])
        for q4 in range(4):
            load("pool", "wo", wo[:, q4 * 4:(q4 + 1) * 4, :], w_o[:, q4 * 4:(q4 + 1) * 4, :], ("wo", q4))
        bc_load(G1, g_post_mix, "p4G")
        bc_load(tg, mod_row(2), "p4tg", reads=["mod_d"])
        P.op("dve", lambda e: e.tensor_tensor(out=G1, in0=G1, in1=tg, op=ALU.mult), reads=["p4G", "p4tg"], writes=["p4G"])
    for t in range(NT4):
        n = 16 if t == 0 else 128
        c0 = 0 if t == 0 else 16 + (t - 1) * 128
        r0 = c0
        sx = t % 2
        bset = [0, 1, 2, 3] if t % 2 == 0 else [4, 5, 6, 7]
        load("sp", ("p4xt", sx), xt4[sx][0:n], x_own[r0:r0 + n, :], ("p4xt", sx))
        for cg in range(4):
            bi = bset[cg]
            for k in range(16):
                mm(bi, banks[bi][0:n, :], mc[:, k, c0:c0 + n], wo[:, k, cg * 512:(cg + 1) * 512], k == 0, k == 15,
                   [("mc", k // 8), ("wo", k // 4)])
        col = 8 + 8 * sx
        for cg in range(4):
            bi = bset[cg]
            P.op("act", lambda e, bi=bi, cg=cg, n=n, col=col: e.activation(out=junk[0:n, 0:512], in_=banks[bi][0:n, :], func=AF.Square,
                                                                             accum_out=st[0:n, col + cg:col + cg + 1]),
                 writes=["junk", ("p4ss", sx), BK(bi)])
        P.op("dve", lambda e, n=n, col=col: e.reduce_sum(out=st[0:n, col + 4:col + 5], in_=st[0:n, col:col + 4], axis=AX.X),
             reads=[("p4ss", sx)], writes=[("p4s1", sx)])
        rstd_from(st[0:n, col + 5:col + 6], st[0:n, col + 4:col + 5], 1.0 / D, parts=n, reads=[("p4s1", sx)], wtok=("p4rs", sx))
        for cg in range(4):
            bi = bset[cg]
            P.op("dve", lambda e, bi=bi, cg=cg, n=n, col=col, sx=sx: e.scalar_tensor_tensor(
                out=x1t[sx][0:n, cg * 512:(cg + 1) * 512], in0=banks[bi][0:n, :], scalar=st[0:n, col + 5:col + 6],
                in1=G1[0:n, cg * 512:(cg + 1) * 512], op0=ALU.mult, op1=ALU.mult),
                reads=[("p4rs", sx), "p4G"], writes=[("p4x1", sx), BK(bi)])
        P.op("pool", lambda e, n=n, sx=sx: e.tensor_tensor(out=x1t[sx][0:n], in0=x1t[sx][0:n], in1=xt4[sx][0:n], op=ALU.add),
             reads=[("p4x1", sx), ("p4xt", sx)], writes=[("p4x1", sx)])
        store("pool", "x1w", x1_d[r0:r0 + n, :], x1t[sx][0:n], ("p4x1", sx), "x1_d")
    A.release(m4)

    m5 = A.mark()
    NG5 = 4 if stop_after >= 6 else 0
    if NG5:
        load("sp", "g2c", g2c, mod_d[0:1, 5 * D:6 * D].rearrange("o (k p) -> p (o k)", p=128), "g2c", reads=["mod_d"], slow=True)
        P.op("dve", lambda e: e.tensor_tensor(out=g2c, in0=g2c, in1=gpf, op=ALU.mult), reads=["g2c", "gpf"], writes=["g2c"])
    nt5 = NT(None, None, "p5", depth=2)
    h2T = A.alloc("h2T", [16, 512], BF16)
    h2Th = A.alloc("h2Th", [16, 16], BF16)
    uh = A.alloc("uh", [88, 16], F32)
    hvalf = hval.rearrange("p a b -> p (a b)")
    wur = [A.alloc("wur", [16, 128], BF16) for s in range(4)]
    ua = [A.alloc("ua", [516], F32) for s in range(2)]
    ug = [A.alloc("ug", [516], F32) for s in range(2)]
    ya = [A.alloc("ya", [512], F32) for s in range(2)]
    yg = [A.alloc("yg", [512], F32) for s in range(2)]
    actT = A.alloc("actT", [44, 512], BF16)
    wdr = [A.alloc("wdr", [44, 128], BF16) for s in range(2)]
    sq5 = [A.alloc("p5sq", [512], BF16) for s in range(3)]
    rbc5 = A.alloc("rbc5", [512], F32)
    x1r = nt5.xt
    upb = Rot([2, 3, 4, 5])
    wui = 0
    wdi = 0
    for m in range(NG5):
        mg = A.mark()
        A2 = A.alloc("p5A", [D], F32)
        B2 = A.alloc("p5B", [D], F32)
        bc_load(A2, g_pre_ffn, "p5A")
        bc_load(B2, mod_row(4), "p5B", reads=["mod_d"])
        P.op("dve", lambda e, A2=A2, B2=B2: e.scalar_tensor_tensor(out=A2, in0=B2, scalar=1.0, in1=A2, op0=ALU.add, op1=ALU.mult),
             reads=["p5A", "p5B"], writes=["p5A"])
        bc_load(B2, mod_row(3), "p5B", reads=["mod_d"])
        nt5.Abc, nt5.Bbc = A2, B2
        if m == 0:
            nt5.tile(x1_d[0:16, :], 16, h2Th, "h2Th", 0)
        for i in range(4):
            r0 = 16 + m * 512 + i * 128
            nt5.tile(x1_d[r0:r0 + 128, :], 128, h2T, "h2T", i * 128)
        nt5.flush()
        A.release(mg)
        ysT = A.alloc("ysT", [16, 512], F32)
        for i in range(44):
            es = i % 2
            sl = []
            for j in range(2):
                s = wui % 4
                wui += 1
                load("sp", "w", wur[s], wupb[2 * i + j].rearrange("p (k c) -> p k c", k=16), ("wur", s), reads=[("wupb", 2 * i + j)])
                sl.append(s)
            for j, (ub, UT) in enumerate(((ua[es], ("ua", es)), (ug[es], ("ug", es)))):
                s = sl[j]
                bi = upb.next()
                for k in range(16):
                    mm(bi, banks[bi][:, 0:512], wur[s][:, k, :], h2T[:, k, :], k == 0, k == 15, [("wur", s), "h2T"])
                P.op("act", lambda e, bi=bi, ub=ub: e.copy(out=ub[:, 4:516], in_=banks[bi][:, 0:512]), writes=[UT, BK(bi)])
                cidx = 2 * i + j
                if m == 0:
                    bi2 = upb.next()
                    for k in range(16):
                        mm(bi2, banks[bi2][:, 0:16], wur[s][:, k, :], h2Th[:, k, :], k == 0, k == 15, [("wur", s), "h2Th"])
                    P.op("dve", lambda e, bi2=bi2, cidx=cidx: e.tensor_tensor(out=uh[:, cidx, :], in0=banks[bi2][:, 0:16], in1=hvalf, op=ALU.mult),
                         reads=["hval"], writes=[("uh", cidx), BK(bi2)])
                P.op("pool", lambda e, ub=ub, cidx=cidx, m=m: e.tensor_copy(out=ub[:, 0:4], in_=uh[:, cidx, 4 * m:4 * m + 4]),
                     reads=[("uh", cidx)], writes=[UT])
            for (ub, UT, yb, YT, cw, cb) in ((ua[es], ("ua", es), ya[es], ("ya", es), cwa, cba), (ug[es], ("ug", es), yg[es], ("yg", es), cwg, cbg)):
                P.op("dve", lambda e, ub=ub, yb=yb, cw=cw, cb=cb, i=i: e.tensor_scalar(out=yb, in0=ub[:, 4:516], scalar1=cw[:, i, 2:3], scalar2=cb[:, i:i + 1],
                                                                                        op0=ALU.mult, op1=ALU.add), reads=[UT, "cwa", "cwg", "cba", "cbg"], writes=[YT])
                P.op("dve", lambda e, ub=ub, yb=yb, cw=cw, i=i: e.scalar_tensor_tensor(out=yb, in0=ub[:, 3:515], scalar=cw[:, i, 1:2], in1=yb,
                                                                                         op0=ALU.mult, op1=ALU.add), reads=[UT, YT, "cwa", "cwg"], writes=[YT])
                P.op("dve", lambda e, ub=ub, yb=yb, cw=cw, i=i: e.scalar_tensor_tensor(out=yb, in0=ub[:, 2:514], scalar=cw[:, i, 0:1], in1=yb,
                                                                                        op0=ALU.mult, op1=ALU.add), reads=[UT, YT, "cwa", "cwg"], writes=[YT])
            P.op("act", lambda e, es=es: e.activation(out=yg[es], in_=yg[es], func=AF.Silu), reads=[("yg", es)], writes=[("yg", es)])
            P.op("pool", lambda e, es=es, i=i: e.tensor_tensor(out=actT[:, i, :], in0=yg[es], in1=ya[es], op=ALU.mult),
                 reads=[("yg", es), ("ya", es)], writes=[("actT", i)])
        for oc in range(16):
            s = wdi % 2
            wdi += 1
            load("sp", "w", wdr[s], wdnb[oc].rearrange("p (k c) -> p k c", k=44), ("wdr", s), reads=[("wdnb", oc)])
            bi = upb.next()
            for kk in range(44):
                mm(bi, banks[bi], wdr[s][:, kk, :], actT[:, kk, :], kk == 0, kk == 43, [("wdr", s), ("actT", kk)])
            P.op("act", lambda e, bi=bi, oc=oc, ysT=ysT: e.copy(out=ysT[:, oc, :], in_=banks[bi]), writes=[("ysT", oc), BK(bi)])
            P.op("dve", lambda e, oc=oc, ysT=ysT: e.tensor_tensor(out=sq5[oc % 3], in0=ysT[:, oc, :], in1=ysT[:, oc, :], op=ALU.mult),
                 reads=[("ysT", oc)], writes=[("p5sq", oc % 3)])
            if oc > 0:
                mm(6, banks[6], ones, sq5[(oc - 1) % 3], oc == 1, False, ["ones", ("p5sq", (oc - 1) % 3)])
        mm(6, banks[6], ones, sq5[15 % 3], False, True, ["ones", ("p5sq", 15 % 3)])
        rstd_from(rbc5, banks[6], 1.0 / D, wtok="rbc5", xw=[BK(6)])
        for oc in range(16):
            P.op("dve", lambda e, oc=oc, ysT=ysT: e.scalar_tensor_tensor(out=ysT[:, oc, :], in0=ysT[:, oc, :], scalar=g2c[:, oc:oc + 1], in1=rbc5,
                                                                          op0=ALU.mult, op1=ALU.mult), reads=[("ysT", oc), "g2c", "rbc5"], writes=[("ysT", oc)])
        for i in range(4):
            sx = i % 2
            XR = ("p5xt", sx)
            r0 = 16 + m * 512 + i * 128
            load("sp", "x", x1r[sx], x1_d[r0:r0 + 128, :], XR, reads=["x1_d"])
            for cg in range(4):
                bi = cg % 2
                for jj in range(4):
                    oc = cg * 4 + jj
                    P.op("pe", lambda e, bi=bi, jj=jj, oc=oc, i=i, ysT=ysT: e.transpose(out=banks[bi][:, jj * 128:(jj + 1) * 128],
                                                                                         in_=ysT[:, oc, i * 128:(i + 1) * 128], identity=identf),
                         reads=[("ysT", oc), "identf"], writes=[BK(bi)])
                P.op("dve", lambda e, bi=bi, cg=cg, sx=sx: e.tensor_tensor(out=x1r[sx][:, cg * 512:(cg + 1) * 512], in0=banks[bi],
                                                                           in1=x1r[sx][:, cg * 512:(cg + 1) * 512], op=ALU.add),
                     reads=[XR], writes=[XR, BK(bi)])
            orow = m * 512 + i * 128
            store("pool", "outw", out[orow:orow + 128, :], x1r[sx], XR)
        A.release(mg)
    A.release(m5)
    counts = P.emit(final_chans=out_chans)
    return nc, dict(counts=counts, n_ops=len(P.ops), peak=A.peak, split={k: (len(v), max(v)) for k, v in P.split.items()})


def _chunk_cols(W, cols):
    Wc = W[:, cols]
    K = Wc.shape[0]
    return np.ascontiguousarray(Wc.reshape(K // 128, 128, -1).transpose(1, 0, 2))


def _vec_pk(v):
    return np.ascontiguousarray(v.reshape(-1, 128).T)


def prepare_inputs(x, c, positions, w_ada, b_ada, g_pre_mix, g_post_mix, w_in, g_q, w_uq, g_kv, w_ukv, conv_w_mix,
                   conv_b_mix, w_o, g_pre_ffn, g_post_ffn, w_up, conv_w_ffn, conv_b_ffn, w_down):
    f32 = np.float32
    x = np.asarray(x, f32)
    positions = np.asarray(positions, np.int32)
    w_in0 = np.asarray(w_in[0], f32)
    ar = np.arange
    kr = 1280 + ar(64)
    kr_sw = 1280 + np.concatenate([ar(32, 64), ar(0, 32)])
    shared = {}
    shared["w_ada"] = np.ascontiguousarray(np.asarray(w_ada[0], f32).reshape(16, 128, 24, 512).transpose(2, 1, 0, 3))
    shared["b_ada"] = np.asarray(b_ada, f32).reshape(1, -1)
    shared["g_pre_mix"] = np.ascontiguousarray(np.broadcast_to(np.asarray(g_pre_mix, f32).reshape(1, -1), (128, D)))
    shared["g_post_mix"] = np.ascontiguousarray(np.broadcast_to(np.asarray(g_post_mix, f32).reshape(1, -1), (128, D)))
    shared["g_pre_ffn"] = np.ascontiguousarray(np.broadcast_to(np.asarray(g_pre_ffn, f32).reshape(1, -1), (128, D)))
    shared["g_post_ffn"] = _vec_pk(np.asarray(g_post_ffn[0], f32))
    shared["w_kv"] = _chunk_cols(w_in0, np.concatenate([768 + ar(512), kr, kr_sw]))
    shared["w_q"] = _chunk_cols(w_in0, ar(768))
    conv_cols = []
    for i in range(8):
        for base in (1344, 2368, 3392):
            conv_cols.append(_chunk_cols(w_in0, base + i * 128 + ar(128)))
    shared["w_conv"] = np.stack(conv_cols)
    wuq0 = np.asarray(w_uq[0], f32)
    uq = []
    for h in range(8):
        cols = np.concatenate([h * 192 + ar(128), h * 192 + 128 + ar(64), h * 192 + 128 + np.concatenate([ar(32, 64), ar(0, 32)])])
        uq.append(_chunk_cols(wuq0, cols))
    shared["w_uq"] = np.stack(uq)
    wukv0 = np.asarray(w_ukv[0], f32)
    shared["w_uk"] = _chunk_cols(wukv0, np.concatenate([h * 256 + ar(128) for h in range(8)]))
    shared["w_uv"] = _chunk_cols(wukv0, np.concatenate([h * 256 + 128 + ar(128) for h in range(8)]))
    shared["w_o"] = _chunk_cols(np.asarray(w_o[0], f32), ar(D))
    wup0 = np.asarray(w_up[0], f32)
    ups = []
    for i in range(44):
        ups.append(_chunk_cols(wup0, i * 128 + ar(128)))
        ups.append(_chunk_cols(wup0, DFF + i * 128 + ar(128)))
    shared["w_up"] = np.stack(ups)
    wd0 = np.asarray(w_down[0], f32)
    shared["w_down"] = np.stack([_chunk_cols(wd0, oc * 128 + ar(128)) for oc in range(16)])
    shared["g_q"] = _vec_pk(np.asarray(g_q[0], f32))
    shared["g_kv"] = _vec_pk(np.asarray(g_kv[0], f32))
    cwm = np.asarray(conv_w_mix[0], f32)
    shared["cw_mix"] = np.ascontiguousarray(cwm.T.reshape(8, 128, 3).transpose(1, 0, 2))
    shared["cb_mix"] = _vec_pk(np.asarray(conv_b_mix[0], f32))
    cwf = np.asarray(conv_w_ffn[0], f32)
    shared["cw_a"] = np.ascontiguousarray(cwf[:, :DFF].T.reshape(44, 128, 3).transpose(1, 0, 2))
    shared["cw_g"] = np.ascontiguousarray(cwf[:, DFF:].T.reshape(44, 128, 3).transpose(1, 0, 2))
    cbf = np.asarray(conv_b_ffn[0], f32)
    shared["cb_a"] = _vec_pk(cbf[:DFF])
    shared["cb_g"] = _vec_pk(cbf[DFF:])
    invf = (1.0 / (10000.0 ** (np.arange(0, 64, 2, dtype=np.float32) / np.float32(64)))).astype(f32)
    shared["invf"] = np.concatenate([invf, invf]).reshape(64, 1).astype(f32)
    shared["sgn"] = np.concatenate([-np.ones(32, f32), np.ones(32, f32)]).reshape(64, 1)
    in_maps = []
    for core in range(8):
        b, j = core // 4, core % 4
        d = dict(shared)
        d["x_all"] = np.ascontiguousarray(x[b])
        xo = np.zeros((HO, D), f32)
        po = np.zeros((HO,), np.int32)
        hv = np.zeros((16,), f32)
        mh = np.zeros((128, 64, 16), f32)
        kidx = (np.arange(64)[None, :] * 128 + np.arange(128)[:, None])
        for m in range(4):
            gb = 4 * m + j
            t0 = gb * 512
            xo[16 + m * 512:16 + (m + 1) * 512] = x[b, t0:t0 + 512]
            po[16 + m * 512:16 + (m + 1) * 512] = positions[b, t0:t0 + 512]
            if gb > 0:
                xo[4 * m:4 * m + 4] = x[b, t0 - 4:t0]
                po[4 * m:4 * m + 4] = positions[b, t0 - 4:t0]
                hv[4 * m:4 * m + 4] = 1.0
                for i in range(4):
                    mh[:, :, 4 * m + i] = np.where(kidx <= t0 - 4 + i, 0.0, NEG)
        d["x_own"] = xo
        d["pos_own"] = np.ascontiguousarray(np.broadcast_to(po[None, :], (64, HO)))
        d["pos_all"] = np.ascontiguousarray(np.broadcast_to(positions[b][None, :], (64, S)))
        d["c_t"] = _vec_pk(np.asarray(c[b], f32))
        d["hval"] = np.ascontiguousarray(np.broadcast_to(hv[None, :], (128, 16)))
        kk = np.arange(16)[None, :, None] * 128 + np.arange(128)[:, None, None]
        qq = j * 512 + np.arange(512)[None, None, :]
        d["mask_own"] = np.where(kk <= qq, 0.0, NEG).astype(ml_dtypes.bfloat16)
        d["mask_halo"] = mh.astype(ml_dtypes.bfloat16)
        in_maps.append(d)
    return in_maps


def assemble(results):
    outp = np.zeros((2, S, D), np.float32)
    for core in range(8):
        b, j = core // 4, core % 4
        o = results[core]["out"]
        for m in range(4):
            gb = 4 * m + j
            outp[b, gb * 512:(gb + 1) * 512] = o[m * 512:(m + 1) * 512]
    return outp


def kernel(**inputs):
    in_maps = prepare_inputs(**inputs)
    nc, info = build()
    res = run_bass_kernel_spmd(nc, in_maps, core_ids=list(range(8)))
    return assemble(res.results)
```

```python
import numpy as np
import ml_dtypes
import concourse.bass as bass
import concourse.mybir as mybir
from concourse.bass_utils import run_bass_kernel_spmd

F32 = mybir.dt.float32
BF16 = mybir.dt.bfloat16
I32 = mybir.dt.int32
U8 = mybir.dt.uint8
ALU = mybir.AluOpType
AF = mybir.ActivationFunctionType
AX = mybir.AxisListType
ENGS = ("pe", "act", "dve", "pool", "sp")

D = 2048
S = 8192
HO = 2064
DFF = 5632
NEG = -30000.0
PI = float(np.pi)
TWO_PI = float(2 * np.pi)
ARENA = 204 * 1024


class Op:
    __slots__ = ("eng", "fn", "deps", "inc", "semval", "dma", "chan", "key", "raw")

    def __init__(self):
        self.inc = False
        self.semval = None
        self.dma = False
        self.chan = None


def _base(t):
    return t if isinstance(t, str) else t[0]


class Prog:
    def __init__(self, nc):
        self.nc = nc
        self.ops = []
        self.lastw = {}
        self.lastr = {}
        self.chan_cnt = {}
        self.uid = 0
        self.bytok = {}
        self.alias_deps = {}
        self.split = {}

    def alias(self, new, old):
        ops = {}
        for t in self.bytok.get(old, ()):
            w = self.lastw.get(t)
            if w is not None:
                ops[id(w)] = w
            for r in self.lastr.get(t, {}).values():
                ops[id(r)] = r
        if ops:
            self.alias_deps.setdefault(new, {}).update(ops)

    def _add(self, op, reads, writes):
        deps = {}
        raw = set()
        for t in reads:
            w = self.lastw.get(t)
            if w is not None:
                deps[id(w)] = w
                raw.add(id(w))
        for t in writes:
            w = self.lastw.get(t)
            if w is not None:
                deps[id(w)] = w
            for r in self.lastr.get(t, {}).values():
                deps[id(r)] = r
        for t in list(reads) + list(writes):
            b = _base(t)
            self.bytok.setdefault(b, set()).add(t)
            ad = self.alias_deps.get(b)
            if ad:
                deps.update(ad)
        deps.pop(id(op), None)
        op.deps = list(deps.values())
        op.raw = raw
        for t in reads:
            self.lastr.setdefault(t, {})[op.key] = op
        for t in writes:
            self.lastw[t] = op
            self.lastr[t] = {}
        self.ops.append(op)
        return op

    def op(self, eng, fn, reads=(), writes=()):
        o = Op()
        o.eng = eng
        o.fn = fn
        o.key = eng
        return self._add(o, reads, writes)

    def dma(self, queue, chan, fn, reads=(), writes=()):
        o = Op()
        o.eng = queue
        o.fn = fn
        o.dma = True
        o.chan = chan
        self.uid += 1
        o.key = ("dma", self.uid)
        self.chan_cnt[chan] = self.chan_cnt.get(chan, 0) + 1
        o.semval = 16 * self.chan_cnt[chan]
        return self._add(o, reads, writes)

    def emit(self, final_chans=()):
        nc = self.nc
        engobj = {"pe": nc.tensor, "act": nc.scalar, "dve": nc.vector, "pool": nc.gpsimd, "sp": nc.sync}

        def skip(op, d):
            return (not op.dma) and (not d.dma) and d.eng == op.eng and (op.eng == "pe" or id(d) not in op.raw)

        for op in self.ops:
            for d in op.deps:
                if d.dma or skip(op, d):
                    continue
                d.inc = True
        sems = {e: nc.alloc_semaphore("s_" + e) for e in ENGS}
        csems = {c: nc.alloc_semaphore("c_%d" % i) for i, c in enumerate(self.chan_cnt)}
        counts = {e: 0 for e in ENGS}
        waited = {}
        for op in self.ops:
            e = engobj[op.eng]
            need = {}
            for d in op.deps:
                if d.dma:
                    k = ("c", d.chan)
                    v = 16 * self.chan_cnt[d.chan] if d.chan in ("const", "pc") else d.semval
                elif skip(op, d):
                    continue
                else:
                    k = ("e", d.eng)
                    v = d.semval
                if need.get(k, 0) < v:
                    need[k] = v
            for k, v in need.items():
                wk = (op.eng, k)
                if waited.get(wk, 0) < v:
                    e.wait_ge(csems[k[1]] if k[0] == "c" else sems[k[1]], v)
                    waited[wk] = v
            if op.dma:
                try:
                    n0 = nc.n_instructions
                    n0 = n0() if callable(n0) else n0
                except Exception:
                    n0 = None
            ins = op.fn(e)
            if op.dma:
                if n0 is not None:
                    n1 = nc.n_instructions
                    n1 = n1() if callable(n1) else n1
                    if n1 - n0 != 1:
                        self.split.setdefault(str(op.chan), []).append(n1 - n0)
                ins.then_inc(csems[op.chan], 16)
            elif op.inc:
                counts[op.eng] += 1
                op.semval = counts[op.eng]
                ins.then_inc(sems[op.eng], 1)
        for c in final_chans:
            nc.sync.wait_ge(csems[c], 16 * self.chan_cnt[c])
        return counts


class Arena:
    def __init__(self, nc, P):
        self.ap = nc.alloc_sbuf_tensor("arena", [128, ARENA], U8).ap()
        self.top = 0
        self.hist = []
        self.P = P
        self.peak = 0

    def alloc(self, name, shape, dtype, parts=128):
        n = int(np.prod(shape)) * mybir.dt.size(dtype)
        start = (self.top + 63) // 64 * 64
        end = start + n
        assert end <= ARENA, (name, end)
        self.top = end
        self.peak = max(self.peak, end)
        for (nm, s, e) in self.hist:
            if s < end and e > start and nm != name:
                self.P.alias(name, nm)
        self.hist.append((name, start, end))
        v = self.ap[0:parts, start:end].bitcast(dtype)
        if len(shape) == 2:
            v = v.rearrange("p (a b) -> p a b", a=shape[0])
        elif len(shape) == 3:
            v = v.rearrange("p (a b c) -> p a b c", a=shape[0], b=shape[1])
        return v

    def mark(self):
        return self.top

    def release(self, m):
        self.top = m


class Rot:
    def __init__(self, items):
        self.items = list(items)
        self.i = 0

    def next(self):
        v = self.items[self.i % len(self.items)]
        self.i += 1
        return v


def build(stop_after=9):
    nc = bass.Bass("TRN2", target_bir_lowering=False)

    def din(name, shape, dt=F32):
        return nc.dram_tensor(name, list(shape), dt, kind="ExternalInput").ap()

    x_all = din("x_all", [S, D])
    x_own = din("x_own", [HO, D])
    pos_all = din("pos_all", [64, S], I32)
    pos_own = din("pos_own", [64, HO], I32)
    c_t = din("c_t", [128, 16])
    w_ada = din("w_ada", [24, 128, 16, 512])
    b_ada = din("b_ada", [1, 6 * D])
    g_pre_mix = din("g_pre_mix", [128, D])
    g_post_mix = din("g_post_mix", [128, D])
    g_pre_ffn = din("g_pre_ffn", [128, D])
    g_post_ffn = din("g_post_ffn", [128, 16])
    w_kv = din("w_kv", [128, 16, 640])
    w_q = din("w_q", [128, 16, 768])
    w_conv = din("w_conv", [24, 128, 16, 128])
    w_uq = din("w_uq", [8, 128, 6, 256])
    w_uk = din("w_uk", [128, 4, 1024])
    w_uv = din("w_uv", [128, 4, 1024])
    w_o = din("w_o", [128, 16, D])
    w_up = din("w_up", [88, 128, 16, 128])
    w_down = din("w_down", [16, 128, 44, 128])
    g_q = din("g_q", [128, 6])
    g_kv = din("g_kv", [128, 4])
    cw_mix = din("cw_mix", [128, 8, 3])
    cb_mix = din("cb_mix", [128, 8])
    cw_a = din("cw_a", [128, 44, 3])
    cw_g = din("cw_g", [128, 44, 3])
    cb_a = din("cb_a", [128, 44])
    cb_g = din("cb_g", [128, 44])
    invf_d = din("invf", [64, 1])
    sgn_d = din("sgn", [64, 1])
    hval_d = din("hval", [128, 16])
    mask_own_d = din("mask_own", [128, 16, 512], BF16)
    mask_halo_d = din("mask_halo", [128, 64, 16], BF16)
    out = nc.dram_tensor("out", [2048, D], F32, kind="ExternalOutput").ap()

    def dscr(name, shape, dt):
        return nc.dram_tensor(name, list(shape), dt, kind="Internal").ap()

    mod_d = dscr("mod_d", [128, 6 * D], F32)
    kT_d = dscr("kT_d", [8, 128, S], BF16)
    v_d = dscr("v_d", [S, 1024], BF16)
    kr_d = dscr("kr_d", [64, S], BF16)
    co_d = dscr("co_d", [8, 128, HO], BF16)
    ao_d = dscr("ao_d", [8, 128, HO], BF16)
    x1_d = dscr("x1_d", [HO, D], F32)

    P = Prog(nc)
    A = Arena(nc, P)
    banks = [nc.alloc_psum_tensor("b%d" % i, [128, 512], F32).ap() for i in range(8)]

    def BK(i):
        return ("bank", i)

    def mm(bi, out_ap, lhsT, rhs, start, stop, reads):
        P.op("pe", lambda e: e.matmul(out_ap, lhsT=lhsT, rhs=rhs, start=start, stop=stop), reads=reads, writes=[BK(bi)])

    out_chans = []

    def load(queue, chan, dst, src, tok, reads=(), slow=False):
        w = [tok]
        if chan == "const":
            ch = "const"
        elif chan in ("bc", "mcl", "wo", "mk"):
            ch = chan
            w.append(("ord", chan))
        else:
            ch = ("L", tok)
        if slow:
            P.dma(queue, ch, lambda e: e.dma_start(out=dst, in_=src, allow_slow_non_contiguous=True), reads=reads, writes=w)
        else:
            P.dma(queue, ch, lambda e: e.dma_start(out=dst, in_=src), reads=reads, writes=w)

    def store(queue, chan, dst, src, rtok, wtok=None):
        ch = ("S", rtok)
        if chan == "outw" and ch not in out_chans:
            out_chans.append(ch)
        P.dma(queue, ch, lambda e: e.dma_start(out=dst, in_=src), reads=[rtok], writes=([wtok] if wtok else []))

    ident = A.alloc("ident", [128], BF16)
    identf = A.alloc("identf", [128], F32)
    ones = A.alloc("ones", [128], BF16)
    onesf = A.alloc("onesf", [128], F32)
    eps = A.alloc("eps", [1], F32)
    gkv = A.alloc("gkv", [4], F32)
    gq = A.alloc("gq", [6], F32)
    cwm = A.alloc("cwm", [8, 3], F32)
    cbm = A.alloc("cbm", [8], F32)
    cwa = A.alloc("cwa", [44, 3], F32)
    cwg = A.alloc("cwg", [44, 3], F32)
    cba = A.alloc("cba", [44], F32)
    cbg = A.alloc("cbg", [44], F32)
    gpf = A.alloc("gpf", [16], F32)
    g2c = A.alloc("g2c", [16], F32)
    invf = A.alloc("invf", [1], F32, parts=64)
    sgn = A.alloc("sgn", [1], F32, parts=64)
    hval = A.alloc("hval", [4, 4], F32)
    junk = A.alloc("junk", [2048], BF16)
    st = A.alloc("st", [64], F32)

    P.op("pool", lambda e: e.memset(identf, 0.0), writes=["identf"])
    P.op("pool", lambda e: e.affine_select(out=identf, in_=identf, pattern=[[-1, 128]], compare_op=ALU.not_equal,
                                            fill=1.0, base=0, channel_multiplier=1), reads=["identf"], writes=["identf"])
    P.op("dve", lambda e: e.tensor_copy(out=ident, in_=identf), reads=["identf"], writes=["ident"])
    P.op("dve", lambda e: e.memset(ones, 1.0), writes=["ones"])
    P.op("dve", lambda e: e.memset(onesf, 1.0), writes=["onesf"])
    P.op("dve", lambda e: e.memset(eps, 1e-6), writes=["eps"])
    for dst, src, nm in [(gkv, g_kv, "gkv"), (gq, g_q, "gq"), (cwm, cw_mix, "cwm"), (cbm, cb_mix, "cbm"), (cwa, cw_a, "cwa"),
                         (cwg, cw_g, "cwg"), (cba, cb_a, "cba"), (cbg, cb_g, "cbg"), (gpf, g_post_ffn, "gpf"),
                         (invf, invf_d, "invf"), (sgn, sgn_d, "sgn"),
                         (hval, hval_d.rearrange("p (a b) -> p a b", a=4), "hval")]:
        load("sp", "const", dst, src, nm)

    def rstd_from(dst, src, n_inv, parts=128, reads=(), wtok=None, xw=()):
        w = [wtok] if wtok else []
        P.op("act", lambda e: e.activation(out=dst, in_=src, func=AF.Sqrt, scale=n_inv, bias=eps[0:parts, 0:1]),
             reads=list(reads) + ["eps"], writes=w + list(xw))
        P.op("dve", lambda e: e.reciprocal(out=dst, in_=dst), reads=w, writes=w)

    m0 = A.mark()
    cT = A.alloc("cT", [16], F32)
    wad = [A.alloc("wad", [16, 512], F32) for s in range(2)]
    modrow = [A.alloc("modrow", [512], F32, parts=1) for s in range(2)]
    bad = [A.alloc("bad", [512], F32, parts=1) for s in range(2)]
    load("sp", "const", cT, c_t, "cT")
    P.op("act", lambda e: e.activation(out=cT, in_=cT, func=AF.Silu), reads=["cT"], writes=["cT"])
    modt = [A.alloc("modt", [512], F32) for s in range(2)]
    def p0_tail(n):
        s = n % 2
        mm(2 + s, banks[2 + s], onesf[0:1, :], modrow[s], True, True, ["onesf", ("modrow", s)])
        P.op("act", lambda e, s=s: e.copy(out=modt[s], in_=banks[2 + s]), writes=[("modt", s), BK(2 + s)])
        store("pool", "modw", mod_d[:, n * 512:(n + 1) * 512], modt[s], ("modt", s), "mod_d")

    for n in range(24):
        s = n % 2
        load("sp", ("wad", s), wad[s], w_ada[n], ("wad", s))
        load("sp", ("bad", s), bad[s], b_ada[0:1, n * 512:(n + 1) * 512], ("bad", s))
        for k in range(16):
            mm(s, banks[s][0:1, :], cT[:, k:k + 1], wad[s][:, k, :], k == 0, k == 15, ["cT", ("wad", s)])
        P.op("dve", lambda e, s=s: e.tensor_tensor(out=modrow[s], in0=banks[s][0:1, :], in1=bad[s], op=ALU.add),
             reads=[("bad", s)], writes=[("modrow", s), BK(s)])
        if n > 0:
            p0_tail(n - 1)
    p0_tail(23)
    A.release(m0)

    wupb = dscr("wupb", [88, 128, 2048], BF16)
    wdnb = dscr("wdnb", [16, 128, 44 * 128], BF16)
    pc_list = [(wupb[c2], w_up[c2].rearrange("p k c -> p (k c)"), ("wupb", c2)) for c2 in range(88)] + \
              [(wdnb[c2], w_down[c2].rearrange("p k c -> p (k c)"), ("wdnb", c2)) for c2 in range(16)]
    pc_pos = [0]

    def precast(n):
        for _ in range(n):
            if pc_pos[0] >= len(pc_list) or stop_after < 6:
                return
            dst, src, tok = pc_list[pc_pos[0]]
            pc_pos[0] += 1
            P.dma("pool", "pc", lambda e, dst=dst, src=src: e.dma_start(out=dst, in_=src), writes=[tok])

    def bc_load(dst, src_row, tok, reads=()):
        load("sp", "bc", dst, src_row, tok, reads)

    def mod_row(i):
        return mod_d[:, i * D:(i + 1) * D]

    def sincos(posi, N, C, Sp, tmp, tag, eng="dve", defer=False):
        ang, a2s, a2c, ki, kf = tmp
        ops = []

        def add(fn, reads, writes):
            ops.append(lambda: P.op(eng, fn, reads=reads, writes=writes))

        add(lambda e: e.tensor_copy(out=ang, in_=posi), [tag + "posi"], [tag + "ang"])
        add(lambda e: e.tensor_scalar(out=ang, in0=ang, scalar1=invf[:, 0:1], scalar2=None, op0=ALU.mult), [tag + "ang", "invf"], [tag + "ang"])
        fins = []
        for shift, dst, dtok, a2, T2 in ((0.0, Sp, tag + "Sp", a2s, tag + "a2s"), (PI / 2, C, tag + "C", a2c, tag + "a2c")):
            TK = tag + "kf"
            add(lambda e, shift=shift, a2=a2: e.tensor_scalar(out=a2, in0=ang, scalar1=1.0, scalar2=shift, op0=ALU.mult, op1=ALU.add),
                [tag + "ang"], [T2])
            add(lambda e, a2=a2: e.tensor_scalar(out=ki, in0=a2, scalar1=1.0 / TWO_PI, scalar2=None, op0=ALU.mult), [T2], [tag + "ki"])
            add(lambda e: e.tensor_copy(out=kf, in_=ki), [tag + "ki"], [TK])
            add(lambda e: e.tensor_scalar(out=kf, in0=kf, scalar1=-TWO_PI, scalar2=None, op0=ALU.mult), [TK], [TK])
            add(lambda e, a2=a2: e.tensor_tensor(out=a2, in0=a2, in1=kf, op=ALU.add), [TK, T2], [T2])
            add(lambda e, a2=a2: e.tensor_scalar(out=kf, in0=a2, scalar1=PI, scalar2=-TWO_PI, op0=ALU.is_gt, op1=ALU.mult), [T2], [TK])
            add(lambda e, a2=a2: e.tensor_tensor(out=a2, in0=a2, in1=kf, op=ALU.add), [TK, T2], [T2])
            add(lambda e, a2=a2: e.tensor_scalar(out=kf, in0=a2, scalar1=-PI, scalar2=TWO_PI, op0=ALU.is_lt, op1=ALU.mult), [T2], [TK])
            add(lambda e, a2=a2: e.tensor_tensor(out=a2, in0=a2, in1=kf, op=ALU.add), [TK, T2], [T2])
            add(lambda e, a2=a2: e.tensor_scalar(out=a2, in0=a2, scalar1=3.1415925, scalar2=-3.1415925, op0=ALU.min, op1=ALU.max), [T2], [T2])
            fins.append((dst, dtok, a2, T2))

        def finish():
            while ops:
                ops.pop(0)()
            for dst, dtok, a2, T2 in fins:
                P.op("act", lambda e, dst=dst, a2=a2: e.activation(out=dst, in_=a2, func=AF.Sin), reads=[T2], writes=[dtok])
            P.op(eng, lambda e: e.tensor_scalar(out=Sp, in0=Sp, scalar1=sgn[:, 0:1], scalar2=None, op0=ALU.mult),
                 reads=[tag + "Sp", "sgn"], writes=[tag + "Sp"])

        if defer:
            return ops, finish
        finish()
        return None

    evac_rr = Rot(["act", "dve"])

    def evac_copy(dst, src, bi, wtok, eng=None, reads=()):
        eng = eng or evac_rr.next()
        if eng == "act":
            P.op("act", lambda e: e.copy(out=dst, in_=src), reads=reads, writes=[wtok, BK(bi)])
        else:
            P.op("dve", lambda e: e.tensor_copy(out=dst, in_=src), reads=reads, writes=[wtok, BK(bi)])

    tr_banks = Rot([0, 1])

    class NT:
        def __init__(self, Abc, Bbc, tag, depth=3):
            self.Abc, self.Bbc, self.tag = Abc, Bbc, tag
            self.depth = depth
            self.xt = [A.alloc(tag + "xt", [D], F32) for s in range(depth)]
            self.hb = [A.alloc(tag + "hb", [D], BF16) for s in range(depth)]
            self.i = 0
            self.pending = []

        def tile(self, rows_ap, n, dst, dtok, col0):
            tag = self.tag
            s = self.i % self.depth
            self.i += 1
            xt, hb = self.xt[s], self.hb[s]
            Abc, Bbc = self.Abc, self.Bbc
            XT, HB, SQ, RS = (tag + "xt", s), (tag + "hb", s), ("st", s), ("st", 3 + s)
            load("sp", XT, xt[0:n], rows_ap, XT)
            P.op("act", lambda e: e.activation(out=junk[0:n], in_=xt[0:n], func=AF.Square, accum_out=st[0:n, s:s + 1]),
                 reads=[XT], writes=["junk", SQ])
            rstd_from(st[0:n, 3 + s:4 + s], st[0:n, s:s + 1], 1.0 / D, parts=n, reads=[SQ], wtok=RS)
            P.op("dve", lambda e: e.scalar_tensor_tensor(out=xt[0:n], in0=xt[0:n], scalar=st[0:n, 3 + s:4 + s], in1=Abc[0:n],
                                                           op0=ALU.mult, op1=ALU.mult), reads=[XT, RS, tag + "A"], writes=[XT])
            P.op("pool", lambda e: e.tensor_tensor(out=hb[0:n], in0=xt[0:n], in1=Bbc[0:n], op=ALU.add),
                 reads=[XT, tag + "B"], writes=[HB])
            self.pending.append((hb, HB, n, dst, dtok, col0))
            if len(self.pending) >= self.depth:
                self.stage_b()

        def flush(self):
            while self.pending:
                self.stage_b()

        def stage_b(self):
            hb, HB, n, dst, dtok, col0 = self.pending.pop(0)
            for kg in range(4):
                bi = tr_banks.next()
                bb = banks[bi].bitcast(BF16)
                for jj in range(4):
                    kc = kg * 4 + jj
                    P.op("pe", lambda e, jj=jj, kc=kc, bb=bb: e.transpose(out=bb[:, jj * 128:jj * 128 + n], in_=hb[0:n, kc * 128:(kc + 1) * 128],
                                                                            identity=ident[0:n, 0:n]),
                         reads=[HB, "ident"], writes=[BK(bi)])
                src = bb[:, 0:512].rearrange("p (j c) -> p j c", j=4)[:, :, 0:n]
                evac_copy(dst[:, kg * 4:(kg + 1) * 4, col0:col0 + n], src, bi, dtok)

    m1 = A.mark()
    A1 = A.alloc("p1A", [D], F32)
    B1 = A.alloc("p1B", [D], F32)
    tmpb = A.alloc("p1tmpb", [D], F32)
    bc_load(A1, g_pre_mix, "p1A")
    bc_load(tmpb, mod_row(1), "p1tmpb", reads=["mod_d"])
    P.op("dve", lambda e: e.scalar_tensor_tensor(out=A1, in0=tmpb, scalar=1.0, in1=A1, op0=ALU.add, op1=ALU.mult),
         reads=["p1A", "p1tmpb"], writes=["p1A"])
    bc_load(B1, mod_row(0), "p1B", reads=["mod_d"])
    m1b = A.mark()
    nt1 = NT(A1, B1, "p1")
    hT = [A.alloc("hT", [16, 512], BF16) for s in range(2)]
    wkv = A.alloc("wkv", [16, 640], BF16)
    wuk = A.alloc("wuk", [4, 1024], BF16)
    wuv = A.alloc("wuv", [4, 1024], BF16)
    kvl = A.alloc("kvl", [4, 512], F32)
    sq = [A.alloc("sq", [512], BF16) for s in range(4)]
    rbc = A.alloc("rbc", [512], F32)
    kvn = A.alloc("kvn", [4, 512], BF16)
    kst = A.alloc("kst", [8, 512], BF16)
    vst = A.alloc("vst", [4, 1024], BF16)
    posi1 = A.alloc("p1posi", [512], I32, parts=64)
    rtmp = [A.alloc("p1" + nm, [512], (I32 if nm == "ki" else F32), parts=64) for nm in ("ang", "a2s", "a2c", "ki", "kf")]
    C1 = A.alloc("p1C", [512], F32, parts=64)
    S1 = A.alloc("p1Sp", [512], F32, parts=64)
    t1 = A.alloc("p1t1", [512], F32, parts=64)
    t2 = A.alloc("p1t2", [512], F32, parts=64)
    krst = A.alloc("krst", [512], BF16, parts=64)
    load("pool", "wkv", wkv, w_kv, "wkv")
    load("pool", "wuk", wuk, w_uk, "wuk")
    load("pool", "wuv", wuv, w_uv, "wuv")
    mmb = Rot([2, 3, 4, 5])
    NG1 = 16 if stop_after >= 1 else 0
    p1_tiles = [(g, i) for g in range(NG1) for i in range(4)]
    p1_next = [0]

    def p1_submit(k=1):
        for _ in range(k):
            if p1_next[0] < len(p1_tiles):
                g2, i2 = p1_tiles[p1_next[0]]
                p1_next[0] += 1
                r0 = g2 * 512 + i2 * 128
                nt1.tile(x_all[r0:r0 + 128, :], 128, hT[g2 % 2], ("hT", g2 % 2), i2 * 128)
            else:
                nt1.flush()

    p1_submit(6)

    p1_fin = [None]
    p1_sc = [[]]

    def p1_trickle(n=1):
        for _ in range(n):
            if p1_sc[0]:
                p1_sc[0].pop(0)()

    def p1_x(g):
        hs = g % 2
        HT = ("hT", hs)
        if g == NG1 - 1:
            nt1.flush()
        for cc in range(4):
            bi = mmb.next()
            for k in range(16):
                mm(bi, banks[bi], wkv[:, k, cc * 128:(cc + 1) * 128], hT[hs][:, k, :], k == 0, k == 15, ["wkv", HT])
            P.op("act", lambda e, bi=bi, cc=cc: e.copy(out=kvl[:, cc, :], in_=banks[bi]), writes=[("kvl", cc), BK(bi)])
            P.op("dve", lambda e, cc=cc: e.tensor_tensor(out=sq[cc], in0=kvl[:, cc, :], in1=kvl[:, cc, :], op=ALU.mult),
                 reads=[("kvl", cc)], writes=[("sq", cc)])
            p1_trickle(2)
            if cc == 1:
                p1_submit()
        if p1_fin[0] is not None:
            p1_fin[0]()
            p1_fin[0] = None
        for which, c0, bi in ((0, 512, 7), (1, 576, mmb.next())):
            for k in range(16):
                mm(bi, banks[bi][0:64, :], wkv[:, k, c0:c0 + 64], hT[hs][:, k, :], k == 0, k == 15, ["wkv", HT])
            tb = t1 if which == 0 else t2
            tt = C1 if which == 0 else S1
            P.op("dve", lambda e, bi=bi, tb=tb, tt=tt: e.tensor_tensor(out=tb, in0=banks[bi][0:64, :], in1=tt, op=ALU.mult),
                 reads=["p1C" if which == 0 else "p1Sp"], writes=["p1t1" if which == 0 else "p1t2", BK(bi)])
        for cc in range(4):
            mm(6, banks[6], ones, sq[cc], cc == 0, cc == 3, ["ones", ("sq", cc)])
        p1_sincos(g + 1)
        P.op("pool", lambda e: e.tensor_tensor(out=krst, in0=t1, in1=t2, op=ALU.add), reads=["p1t1", "p1t2"], writes=["krst"])
        store("pool", "krw", kr_d[:, g * 512:(g + 1) * 512], krst, "krst", "kr_d")
        p1_submit()

    def p1_sincos(g):
        if g < NG1:
            load("sp", "p1pos", posi1, pos_all[:, g * 512:(g + 1) * 512], "p1posi")
            p1_sc[0], p1_fin[0] = sincos(posi1, 512, C1, S1, rtmp, "p1", eng="dve", defer=True)

    def p1_xtail(g):
        rstd_from(rbc, banks[6], 1.0 / 512, wtok="rbc", xw=[BK(6)])
        for cc in range(4):
            P.op("dve", lambda e, cc=cc: e.scalar_tensor_tensor(out=kvn[:, cc, :], in0=kvl[:, cc, :], scalar=gkv[:, cc:cc + 1], in1=rbc,
                                                                 op0=ALU.mult, op1=ALU.mult),
                 reads=[("kvl", cc), "gkv", "rbc"], writes=["kvn"])

    def p1_y(g):
        for h in range(8):
            bi = mmb.next()
            for k in range(4):
                mm(bi, banks[bi], wuk[:, k, h * 128:(h + 1) * 128], kvn[:, k, :], k == 0, k == 3, ["wuk", "kvn"])
            evac_copy(kst[:, h, :], banks[bi], bi, "kst", eng="act")
            p1_trickle(1)
        store("pool", "kw", kT_d[:, :, g * 512:(g + 1) * 512].rearrange("h p t -> p h t"), kst, "kst", "kT_d")
        p1_submit()
        for i in range(4):
            for hh in range(2):
                bi = mmb.next()
                for k in range(4):
                    mm(bi, banks[bi], kvn[:, k, i * 128:(i + 1) * 128], wuv[:, k, hh * 512:(hh + 1) * 512], k == 0, k == 3, ["wuv", "kvn"])
                evac_copy(vst[:, i, hh * 512:(hh + 1) * 512], banks[bi], bi, "vst", eng="act")
                p1_trickle(1)
        store("pool", "vw", v_d[g * 512:(g + 1) * 512, :].rearrange("(i p) c -> p i c", p=128), vst, "vst", "v_d")
        precast(3)
        p1_submit()

    p1_sincos(0)
    for g in range(NG1):
        p1_x(g)
        if g > 0:
            p1_y(g - 1)
        else:
            p1_submit(2)
        p1_xtail(g)
    if NG1:
        p1_y(NG1 - 1)
    A.release(m1)

    qg = A.alloc("qg", [6, HO], BF16)
    rsq = A.alloc("rsq", [HO], F32)
    mQ = A.mark()
    hTo = A.alloc("hTo", [16, HO], BF16)
    m2 = A.mark()
    A1b = A.alloc("p2A", [D], F32)
    B1b = A.alloc("p2B", [D], F32)
    tmpb2 = A.alloc("p2tmpb", [D], F32)
    bc_load(A1b, g_pre_mix, "p2A")
    bc_load(tmpb2, mod_row(1), "p2tmpb", reads=["mod_d"])
    P.op("dve", lambda e: e.scalar_tensor_tensor(out=A1b, in0=tmpb2, scalar=1.0, in1=A1b, op0=ALU.add, op1=ALU.mult),
         reads=["p2A", "p2tmpb"], writes=["p2A"])
    bc_load(B1b, mod_row(0), "p2B", reads=["mod_d"])
    nt2 = NT(A1b, B1b, "p2")
    wq = A.alloc("wq", [16, 768], BF16)
    load("pool", "wq", wq, w_q, "wq")
    sq2 = [A.alloc("p2sq", [512], BF16) for s in range(2)]
    groups = [(0, 16)] + [(16 + 512 * m, 512) for m in range(4)]
    if stop_after >= 2:
        nt2.tile(x_own[0:16, :], 16, hTo, "hTo", 0)
        for t in range(16):
            nt2.tile(x_own[16 + t * 128:16 + (t + 1) * 128, :], 128, hTo, "hTo", 16 + t * 128)
        nt2.flush()
        precast(8)
        for (c0, N) in groups:
            for cc in range(6):
                bi = mmb.next()
                for k in range(16):
                    mm(bi, banks[bi][:, 0:N], wq[:, k, cc * 128:(cc + 1) * 128], hTo[:, k, c0:c0 + N], k == 0, k == 15, ["wq", "hTo"])
                P.op("act", lambda e, bi=bi, cc=cc, c0=c0, N=N: e.activation(out=qg[:, cc, c0:c0 + N], in_=banks[bi][:, 0:N], func=AF.Identity,
                                                                               scale=gq[:, cc:cc + 1]),
                     reads=["gq"], writes=["qg", BK(bi)])
                P.op("act", lambda e, bi=bi, cc=cc, N=N: e.activation(out=sq2[cc % 2][:, 0:N], in_=banks[bi][:, 0:N], func=AF.Square),
                     writes=[("p2sq", cc % 2), BK(bi)])
                mm(6, banks[6][:, 0:N], ones, sq2[cc % 2][:, 0:N], cc == 0, cc == 5, ["ones", ("p2sq", cc % 2)])
            rstd_from(rsq[:, c0:c0 + N], banks[6][:, 0:N], 1.0 / 768, wtok="rsq", xw=[BK(6)])
    A.release(m2)

    m2b = A.mark()
    wcr = [A.alloc("wcr", [16, 128], BF16) for s in range(6)]
    gcs = A.alloc("gcs", [4, 516], F32)
    gbs = A.alloc("gbs", [4, 516], F32)
    mmx = A.alloc("mmx", [4, 516], F32)
    yy = A.alloc("yy", [4, 516], F32)
    cst = [A.alloc("cst", [HO], BF16) for s in range(2)]
    P.op("dve", lambda e: e.memset(yy, 0.0), writes=["yy"])

    def eb_dst(buf, c0, N):
        if N == 16:
            return buf[:, :, 0:4]
        m = (c0 - 16) // 512
        return buf[:, m, 4:516]

    def eb_src(bi, N):
        if N == 16:
            return banks[bi][:, 0:16].rearrange("p (m i) -> p m i", m=4)
        return banks[bi][:, 0:512]

    NI = 8 if stop_after >= 3 else 0
    wci = 0
    for i in range(NI):
        slots = []
        for j in range(3):
            s = wci % 6
            wci += 1
            load("pool", ("wcr", s), wcr[s], w_conv[i * 3 + j], ("wcr", s))
            slots.append(s)
        for j, kind in ((1, "gc"), (2, "ci"), (0, "gb")):
            s = slots[j]
            for (c0, N) in groups:
                bi = mmb.next()
                for k in range(16):
                    mm(bi, banks[bi][:, 0:N], wcr[s][:, k, :], hTo[:, k, c0:c0 + N], k == 0, k == 15, [("wcr", s), "hTo"])
                if kind == "gc":
                    P.op("act", lambda e, bi=bi, c0=c0, N=N: e.copy(out=eb_dst(gcs, c0, N), in_=eb_src(bi, N)), writes=["gcs", BK(bi)])
                elif kind == "ci":
                    P.op("dve", lambda e, bi=bi, c0=c0, N=N: e.tensor_tensor(out=eb_dst(mmx, c0, N), in0=eb_src(bi, N), in1=eb_dst(gcs, c0, N),
                                                                              op=ALU.mult), reads=["gcs"], writes=["mmx", BK(bi)])
                else:
                    P.op("act", lambda e, bi=bi, c0=c0, N=N: e.copy(out=eb_dst(gbs, c0, N), in_=eb_src(bi, N)), writes=["gbs", BK(bi)])
            if kind == "ci":
                P.op("dve", lambda e: e.tensor_tensor(out=mmx[:, :, 0:4], in0=mmx[:, :, 0:4], in1=hval, op=ALU.mult),
                     reads=["mmx", "hval"], writes=["mmx"])
                P.op("dve", lambda e, i=i: e.tensor_scalar(out=yy[:, :, 2:516], in0=mmx[:, :, 2:516], scalar1=cwm[:, i, 2:3], scalar2=cbm[:, i:i + 1],
                                                            op0=ALU.mult, op1=ALU.add), reads=["mmx", "cwm", "cbm"], writes=["yy"])
                P.op("dve", lambda e, i=i: e.scalar_tensor_tensor(out=yy[:, :, 2:516], in0=mmx[:, :, 1:515], scalar=cwm[:, i, 1:2], in1=yy[:, :, 2:516],
                                                                    op0=ALU.mult, op1=ALU.add), reads=["mmx", "cwm", "yy"], writes=["yy"])
                P.op("dve", lambda e, i=i: e.scalar_tensor_tensor(out=yy[:, :, 2:516], in0=mmx[:, :, 0:514], scalar=cwm[:, i, 0:1], in1=yy[:, :, 2:516],
                                                                   op0=ALU.mult, op1=ALU.add), reads=["mmx", "cwm", "yy"], writes=["yy"])
        cs = i % 2
        P.op("pool", lambda e, cs=cs: e.tensor_tensor(out=cst[cs][:, 0:16].rearrange("p (m i) -> p m i", m=4), in0=gbs[:, :, 0:4], in1=yy[:, :, 0:4],
                                                       op=ALU.mult), reads=["gbs", "yy"], writes=[("cst", cs)])
        P.op("dve", lambda e, cs=cs: e.tensor_tensor(out=cst[cs][:, 16:HO].rearrange("p (m i) -> p m i", m=4), in0=gbs[:, :, 4:516], in1=yy[:, :, 4:516],
                                                      op=ALU.mult), reads=["gbs", "yy"], writes=[("cst", cs)])
        store("pool", "cow", co_d[i], cst[cs], ("cst", cs), "co_d")
        precast(2)
    A.release(mQ)

    m3 = A.mark()
    Cq = A.alloc("p3C", [HO], F32, parts=64)
    Sq = A.alloc("p3Sp", [HO], F32, parts=64)
    m3t = A.mark()
    posi3 = A.alloc("p3posi", [HO], I32, parts=64)
    rtmp3 = [A.alloc("p3" + nm, [HO], (I32 if nm == "ki" else F32), parts=64) for nm in ("ang", "a2s", "a2c", "ki", "kf")]
    NH = 8 if stop_after >= 4 else 0
    if NH:
        load("sp", "p3pos", posi3, pos_own, "p3posi")
        sincos(posi3, HO, Cq, Sq, rtmp3, "p3")
        P.op("dve", lambda e: e.tensor_tensor(out=Cq, in0=Cq, in1=rsq[0:64, :], op=ALU.mult), reads=["p3C", "rsq"], writes=["p3C"])
        P.op("dve", lambda e: e.tensor_tensor(out=Sq, in0=Sq, in1=rsq[0:64, :], op=ALU.mult), reads=["p3Sp", "rsq"], writes=["p3Sp"])
    A.release(m3t)
    Kh = [A.alloc("Kh", [S], BF16) for s in range(2)]
    Vh = [A.alloc("Vh", [64, 128], BF16) for s in range(2)]
    krT = A.alloc("krT", [S], BF16, parts=64)
    wuq = [A.alloc("wuq", [6, 256], BF16) for s in range(2)]
    qn = [A.alloc("qn", [HO], BF16) for s in range(2)]
    qr = [A.alloc("qr", [HO], BF16, parts=64) for s in range(2)]
    qt1 = A.alloc("qt1", [512], F32, parts=64)
    qt2 = A.alloc("qt2", [512], F32, parts=64)
    mko = A.alloc("mko", [16, 512], BF16)
    mkh = A.alloc("mkh", [64, 16], BF16)
    pT = [A.alloc("pT", [512], BF16) for s in range(4)]
    rec = A.alloc("rec", [512], F32)
    dacc = [A.alloc("dacc", [512], F32) for s in range(2)]
    aost = [A.alloc("aost", [HO], BF16) for s in range(1)] * 2
    SCALE = 1.0 / float(np.sqrt(192.0))
    if NH:
        load("sp", "krl", krT, kr_d, "krT", reads=["kr_d"])
        load("sp", "mk", mko, mask_own_d, "mko")
        load("sp", "mk", mkh, mask_halo_d, "mkh")

    def load_head(h):
        s = h % 2
        load("sp", ("Kh", s), Kh[s], kT_d[h], ("Kh", s), reads=["kT_d"])
        for part in range(4):
            load("sp", ("Vh", s), Vh[s][:, part * 16:(part + 1) * 16, :],
                 v_d[part * 2048:(part + 1) * 2048, h * 128:(h + 1) * 128].rearrange("(t p) d -> p t d", p=128),
                 ("Vh", s, part), reads=["v_d"])
        load("pool", ("wuq", s), wuq[s], w_uq[h], ("wuq", s))

    s_banks = Rot([0, 1, 2])
    od_banks = Rot([(3, 4, 0), (5, 6, 1)])
    if NH:
        load_head(0)
    def qproj_pieces(h):
        s = h % 2
        pieces = []
        for (c0, N) in groups:
            def nope(s=s, c0=c0, N=N):
                for k in range(6):
                    mm(7, banks[7][:, 0:N], wuq[s][:, k, 0:128], qg[:, k, c0:c0 + N], k == 0, k == 5, [("wuq", s), "qg"])
                P.op("dve", lambda e: e.tensor_tensor(out=qn[s][:, c0:c0 + N], in0=banks[7][:, 0:N], in1=rsq[:, c0:c0 + N], op=ALU.mult),
                     reads=["rsq"], writes=[("qn", s), BK(7)])

            def rope(which, s=s, c0=c0, N=N):
                w0 = 128 if which == 0 else 192
                for k in range(6):
                    mm(7, banks[7][0:64, 0:N], wuq[s][:, k, w0:w0 + 64], qg[:, k, c0:c0 + N], k == 0, k == 5, [("wuq", s), "qg"])
                tb = qt1 if which == 0 else qt2
                tt = Cq if which == 0 else Sq
                P.op("dve", lambda e: e.tensor_tensor(out=tb[:, 0:N], in0=banks[7][0:64, 0:N], in1=tt[:, c0:c0 + N], op=ALU.mult),
                     reads=["p3C" if which == 0 else "p3Sp"], writes=["qt1" if which == 0 else "qt2", BK(7)])
                if which == 1:
                    P.op("pool", lambda e: e.tensor_tensor(out=qr[s][:, c0:c0 + N], in0=qt1[:, 0:N], in1=qt2[:, 0:N], op=ALU.add),
                         reads=["qt1", "qt2"], writes=[("qr", s)])

            pieces.append(nope)
            pieces.append(lambda rope=rope: rope(0))
            pieces.append(lambda rope=rope: rope(1))
        return pieces

    if NH:
        for pc_ in qproj_pieces(0):
            pc_()
    for h in range(NH):
        s = h % 2
        precast(4)
        if h + 1 < NH:
            load_head(h + 1)
        nxt = qproj_pieces(h + 1) if h + 1 < NH else []
        steps = []
        for gi, (c0, N) in enumerate(groups):
            if gi == 0:
                kts = [(kt, ("h", kt)) for kt in range(64)]
            else:
                m = gi - 1
                nk = 16 * (m + 1)
                kts = [(kt, (("o", kt - 16 * m) if kt >= 16 * m else None)) for kt in range(nk)]
            for idx, (kt, mk) in enumerate(kts):
                steps.append((gi, c0, N, kt, mk, idx == 0, idx == len(kts) - 1))
        cur = {}

        def s_stage(stp):
            gi, c0, N, kt, mk, first, last = stp
            bi = s_banks.next()
            mm(bi, banks[bi][:, 0:N], Kh[s][:, kt * 128:(kt + 1) * 128], qn[s][:, c0:c0 + N], True, False, [("Kh", s), ("qn", s)])
            mm(bi, banks[bi][:, 0:N], krT[:, kt * 128:(kt + 1) * 128], qr[s][:, c0:c0 + N], False, mk is None, ["krT", ("qr", s)])
            if mk is not None:
                mt = mkh[:, mk[1], :] if mk[0] == "h" else mko[:, mk[1], :]
                mm(bi, banks[bi][:, 0:N], ident, mt, False, True, ["ident", "mkh", "mko"])
            return bi

        def pv_stage(stp, bi, pi):
            gi, c0, N, kt, mk, first, last = stp
            if first:
                cur["od"] = od_banks.next()
            bo, bd, da = cur["od"]
            P.op("act", lambda e: e.activation(out=pT[pi][:, 0:N], in_=banks[bi][:, 0:N], func=AF.Exp, scale=SCALE),
                 writes=[("pT", pi), BK(bi)])
            part = kt // 16
            mm(bo, banks[bo][:, 0:N], Vh[s][:, kt, :], pT[pi][:, 0:N], first, last, [("Vh", s, part), ("pT", pi)])
            idx = cur.get("idx", 0) if not first else 0
            cur["idx"] = idx + 1
            if idx % 2 == 0:
                if first:
                    P.op("dve", lambda e: e.tensor_copy(out=dacc[da][:, 0:N], in_=pT[pi][:, 0:N]), reads=[("pT", pi)], writes=[("dacc", da)])
                else:
                    P.op("dve", lambda e: e.tensor_tensor(out=dacc[da][:, 0:N], in0=dacc[da][:, 0:N], in1=pT[pi][:, 0:N], op=ALU.add),
                         reads=[("pT", pi), ("dacc", da)], writes=[("dacc", da)])
            else:
                mm(bd, banks[bd][:, 0:N], ones, pT[pi][:, 0:N], idx == 1, False, ["ones", ("pT", pi)])
            if last:
                mm(bd, banks[bd][:, 0:N], onesf, dacc[da][:, 0:N], False, True, ["onesf", ("dacc", da)])
                P.op("dve", lambda e: e.reciprocal(out=rec[:, 0:N], in_=banks[bd][:, 0:N]), writes=["rec", BK(bd)])
                P.op("dve", lambda e: e.tensor_tensor(out=aost[s][:, c0:c0 + N], in0=banks[bo][:, 0:N], in1=rec[:, 0:N], op=ALU.mult),
                     reads=["rec"], writes=[("aost", 0), BK(bo)])

        LOOK = 2
        sb = {}
        for i2 in range(len(steps) + LOOK):
            if i2 < len(steps):
                sb[i2] = s_stage(steps[i2])
            j2 = i2 - LOOK
            if j2 >= 0:
                pv_stage(steps[j2], sb.pop(j2), j2 % 4)
            if nxt and i2 % 14 == 10:
                nxt.pop(0)()
        while nxt:
            nxt.pop(0)()
        store("pool", "aow", ao_d[h], aost[s], ("aost", 0), "ao_d")
    A.release(m1)

    m4 = A.mark()
    mc = A.alloc("mc", [16, HO], BF16)
    wo = A.alloc("wo", [16, D], BF16)
    G1 = A.alloc("p4G", [D], F32)
    tg = A.alloc("p4tg", [D], F32)
    xt4 = [A.alloc("p4xt", [D], F32) for s in range(2)]
    x1t = [A.alloc("p4x1", [D], F32) for s in range(2)]
    NT4 = 17 if stop_after >= 5 else 0
    if NT4:
        load("sp", "mcl", mc[:, 0:8, :], ao_d.rearrange("h p t -> p h t"), ("mc", 0), reads=["ao_d"])
        load("sp", "mcl", mc[:, 8:16, :], co_d.rearrange("h p t -> p h t"), ("mc", 1), reads=["co_d"])
        for q4 in range(4):
            load("pool", "wo", wo[:, q4 * 4:(q4 + 1) * 4, :], w_o[:, q4 * 4:(q4 + 1) * 4, :], ("wo", q4))
        bc_load(G1, g_post_mix, "p4G")
        bc_load(tg, mod_row(2), "p4tg", reads=["mod_d"])
        P.op("dve", lambda e: e.tensor_tensor(out=G1, in0=G1, in1=tg, op=ALU.mult), reads=["p4G", "p4tg"], writes=["p4G"])
    for t in range(NT4):
        n = 16 if t == 0 else 128
        c0 = 0 if t == 0 else 16 + (t - 1) * 128
        r0 = c0
        sx = t % 2
        bset = [0, 1, 2, 3] if t % 2 == 0 else [4, 5, 6, 7]
        load("sp", ("p4xt", sx), xt4[sx][0:n], x_own[r0:r0 + n, :], ("p4xt", sx))
        for cg in range(4):
            bi = bset[cg]
            for k in range(16):
                mm(bi, banks[bi][0:n, :], mc[:, k, c0:c0 + n], wo[:, k, cg * 512:(cg + 1) * 512], k == 0, k == 15,
                   [("mc", k // 8), ("wo", k // 4)])
        col = 8 + 8 * sx
        for cg in range(4):
            bi = bset[cg]
            P.op("act", lambda e, bi=bi, cg=cg, n=n, col=col: e.activation(out=junk[0:n, 0:512], in_=banks[bi][0:n, :], func=AF.Square,
                                                                             accum_out=st[0:n, col + cg:col + cg + 1]),
                 writes=["junk", ("p4ss", sx), BK(bi)])
        P.op("dve", lambda e, n=n, col=col: e.reduce_sum(out=st[0:n, col + 4:col + 5], in_=st[0:n, col:col + 4], axis=AX.X),
             reads=[("p4ss", sx)], writes=[("p4s1", sx)])
        rstd_from(st[0:n, col + 5:col + 6], st[0:n, col + 4:col + 5], 1.0 / D, parts=n, reads=[("p4s1", sx)], wtok=("p4rs", sx))
        for cg in range(4):
            bi = bset[cg]
            P.op("dve", lambda e, bi=bi, cg=cg, n=n, col=col, sx=sx: e.scalar_tensor_tensor(
                out=x1t[sx][0:n, cg * 512:(cg + 1) * 512], in0=banks[bi][0:n, :], scalar=st[0:n, col + 5:col + 6],
                in1=G1[0:n, cg * 512:(cg + 1) * 512], op0=ALU.mult, op1=ALU.mult),
                reads=[("p4rs", sx), "p4G"], writes=[("p4x1", sx), BK(bi)])
        P.op("pool", lambda e, n=n, sx=sx: e.tensor_tensor(out=x1t[sx][0:n], in0=x1t[sx][0:n], in1=xt4[sx][0:n], op=ALU.add),
             reads=[("p4x1", sx), ("p4xt", sx)], writes=[("p4x1", sx)])
        store("pool", "x1w", x1_d[r0:r0 + n, :], x1t[sx][0:n], ("p4x1", sx), "x1_d")
    A.release(m4)

    m5 = A.mark()
    NG5 = 4 if stop_after >= 6 else 0
    if NG5:
        load("sp", "g2c", g2c, mod_d[0:1, 5 * D:6 * D].rearrange("o (k p) -> p (o k)", p=128), "g2c", reads=["mod_d"], slow=True)
        P.op("dve", lambda e: e.tensor_tensor(out=g2c, in0=g2c, in1=gpf, op=ALU.mult), reads=["g2c", "gpf"], writes=["g2c"])
    nt5 = NT(None, None, "p5", depth=2)
    h2T = A.alloc("h2T", [16, 512], BF16)
    h2Th = A.alloc("h2Th", [16, 16], BF16)
    uh = A.alloc("uh", [88, 16], F32)
    hvalf = hval.rearrange("p a b -> p (a b)")
    wur = [A.alloc("wur", [16, 128], BF16) for s in range(4)]
    ua = [A.alloc("ua", [516], F32) for s in range(2)]
    ug = [A.alloc("ug", [516], F32) for s in range(2)]
    ya = [A.alloc("ya", [512], F32) for s in range(2)]
    yg = [A.alloc("yg", [512], F32) for s in range(2)]
    actT = A.alloc("actT", [44, 512], BF16)
    wdr = [A.alloc("wdr", [44, 128], BF16) for s in range(2)]
    sq5 = [A.alloc("p5sq", [512], BF16) for s in range(3)]
    rbc5 = A.alloc("rbc5", [512], F32)
    x1r = nt5.xt
    upb = Rot([2, 3, 4, 5])
    wui = 0
    wdi = 0
    for m in range(NG5):
        mg = A.mark()
        A2 = A.alloc("p5A", [D], F32)
        B2 = A.alloc("p5B", [D], F32)
        bc_load(A2, g_pre_ffn, "p5A")
        bc_load(B2, mod_row(4), "p5B", reads=["mod_d"])
        P.op("dve", lambda e, A2=A2, B2=B2: e.scalar_tensor_tensor(out=A2, in0=B2, scalar=1.0, in1=A2, op0=ALU.add, op1=ALU.mult),
             reads=["p5A", "p5B"], writes=["p5A"])
        bc_load(B2, mod_row(3), "p5B", reads=["mod_d"])
        nt5.Abc, nt5.Bbc = A2, B2
        if m == 0:
            nt5.tile(x1_d[0:16, :], 16, h2Th, "h2Th", 0)
        for i in range(4):
            r0 = 16 + m * 512 + i * 128
            nt5.tile(x1_d[r0:r0 + 128, :], 128, h2T, "h2T", i * 128)
        nt5.flush()
        A.release(mg)
        ysT = A.alloc("ysT", [16, 512], F32)
        for i in range(44):
            es = i % 2
            sl = []
            for j in range(2):
                s = wui % 4
                wui += 1
                load("sp", "w", wur[s], wupb[2 * i + j].rearrange("p (k c) -> p k c", k=16), ("wur", s), reads=[("wupb", 2 * i + j)])
                sl.append(s)
            for j, (ub, UT) in enumerate(((ua[es], ("ua", es)), (ug[es], ("ug", es)))):
                s = sl[j]
                bi = upb.next()
                for k in range(16):
                    mm(bi, banks[bi][:, 0:512], wur[s][:, k, :], h2T[:, k, :], k == 0, k == 15, [("wur", s), "h2T"])
                P.op("act", lambda e, bi=bi, ub=ub: e.copy(out=ub[:, 4:516], in_=banks[bi][:, 0:512]), writes=[UT, BK(bi)])
                cidx = 2 * i + j
                if m == 0:
                    bi2 = upb.next()
                    for k in range(16):
                        mm(bi2, banks[bi2][:, 0:16], wur[s][:, k, :], h2Th[:, k, :], k == 0, k == 15, [("wur", s), "h2Th"])
                    P.op("dve", lambda e, bi2=bi2, cidx=cidx: e.tensor_tensor(out=uh[:, cidx, :], in0=banks[bi2][:, 0:16], in1=hvalf, op=ALU.mult),
                         reads=["hval"], writes=[("uh", cidx), BK(bi2)])
                P.op("pool", lambda e, ub=ub, cidx=cidx, m=m: e.tensor_copy(out=ub[:, 0:4], in_=uh[:, cidx, 4 * m:4 * m + 4]),
                     reads=[("uh", cidx)], writes=[UT])
            for (ub, UT, yb, YT, cw, cb) in ((ua[es], ("ua", es), ya[es], ("ya", es), cwa, cba), (ug[es], ("ug", es), yg[es], ("yg", es), cwg, cbg)):
                P.op("dve", lambda e, ub=ub, yb=yb, cw=cw, cb=cb, i=i: e.tensor_scalar(out=yb, in0=ub[:, 4:516], scalar1=cw[:, i, 2:3], scalar2=cb[:, i:i + 1],
                                                                                        op0=ALU.mult, op1=ALU.add), reads=[UT, "cwa", "cwg", "cba", "cbg"], writes=[YT])
                P.op("dve", lambda e, ub=ub, yb=yb, cw=cw, i=i: e.scalar_tensor_tensor(out=yb, in0=ub[:, 3:515], scalar=cw[:, i, 1:2], in1=yb,
                                                                                         op0=ALU.mult, op1=ALU.add), reads=[UT, YT, "cwa", "cwg"], writes=[YT])
                P.op("dve", lambda e, ub=ub, yb=yb, cw=cw, i=i: e.scalar_tensor_tensor(out=yb, in0=ub[:, 2:514], scalar=cw[:, i, 0:1], in1=yb,
                                                                                        op0=ALU.mult, op1=ALU.add), reads=[UT, YT, "cwa", "cwg"], writes=[YT])
            P.op("act", lambda e, es=es: e.activation(out=yg[es], in_=yg[es], func=AF.Silu), reads=[("yg", es)], writes=[("yg", es)])
            P.op("pool", lambda e, es=es, i=i: e.tensor_tensor(out=actT[:, i, :], in0=yg[es], in1=ya[es], op=ALU.mult),
                 reads=[("yg", es), ("ya", es)], writes=[("actT", i)])
        for oc in range(16):
            s = wdi % 2
            wdi += 1
            load("sp", "w", wdr[s], wdnb[oc].rearrange("p (k c) -> p k c", k=44), ("wdr", s), reads=[("wdnb", oc)])
            bi = upb.next()
            for kk in range(44):
                mm(bi, banks[bi], wdr[s][:, kk, :], actT[:, kk, :], kk == 0, kk == 43, [("wdr", s), ("actT", kk)])
            P.op("act", lambda e, bi=bi, oc=oc, ysT=ysT: e.copy(out=ysT[:, oc, :], in_=banks[bi]), writes=[("ysT", oc), BK(bi)])
            P.op("dve", lambda e, oc=oc, ysT=ysT: e.tensor_tensor(out=sq5[oc % 3], in0=ysT[:, oc, :], in1=ysT[:, oc, :], op=ALU.mult),
                 reads=[("ysT", oc)], writes=[("p5sq", oc % 3)])
            if oc > 0:
                mm(6, banks[6], ones, sq5[(oc - 1) % 3], oc == 1, False, ["ones", ("p5sq", (oc - 1) % 3)])
        mm(6, banks[6], ones, sq5[15 % 3], False, True, ["ones", ("p5sq", 15 % 3)])
        rstd_from(rbc5, banks[6], 1.0 / D, wtok="rbc5", xw=[BK(6)])
        for oc in range(16):
            P.op("dve", lambda e, oc=oc, ysT=ysT: e.scalar_tensor_tensor(out=ysT[:, oc, :], in0=ysT[:, oc, :], scalar=g2c[:, oc:oc + 1], in1=rbc5,
                                                                          op0=ALU.mult, op1=ALU.mult), reads=[("ysT", oc), "g2c", "rbc5"], writes=[("ysT", oc)])
        for i in range(4):
            sx = i % 2
            XR = ("p5xt", sx)
            r0 = 16 + m * 512 + i * 128
            load("sp", "x", x1r[sx], x1_d[r0:r0 + 128, :], XR, reads=["x1_d"])
            for cg in range(4):
                bi = cg % 2
                for jj in range(4):
                    oc = cg * 4 + jj
                    P.op("pe", lambda e, bi=bi, jj=jj, oc=oc, i=i, ysT=ysT: e.transpose(out=banks[bi][:, jj * 128:(jj + 1) * 128],
                                                                                         in_=ysT[:, oc, i * 128:(i + 1) * 128], identity=identf),
                         reads=[("ysT", oc), "identf"], writes=[BK(bi)])
                P.op("dve", lambda e, bi=bi, cg=cg, sx=sx: e.tensor_tensor(out=x1r[sx][:, cg * 512:(cg + 1) * 512], in0=banks[bi],
                                                                           in1=x1r[sx][:, cg * 512:(cg + 1) * 512], op=ALU.add),
                     reads=[XR], writes=[XR, BK(bi)])
            orow = m * 512 + i * 128
            store("pool", "outw", out[orow:orow + 128, :], x1r[sx], XR)
        A.release(mg)
    A.release(m5)
    counts = P.emit(final_chans=out_chans)
    return nc, dict(counts=counts, n_ops=len(P.ops), peak=A.peak, split={k: (len(v), max(v)) for k, v in P.split.items()})


def _chunk_cols(W, cols):
    Wc = W[:, cols]
    K = Wc.shape[0]
    return np.ascontiguousarray(Wc.reshape(K // 128, 128, -1).transpose(1, 0, 2))


def _vec_pk(v):
    return np.ascontiguousarray(v.reshape(-1, 128).T)


def prepare_inputs(x, c, positions, w_ada, b_ada, g_pre_mix, g_post_mix, w_in, g_q, w_uq, g_kv, w_ukv, conv_w_mix,
                   conv_b_mix, w_o, g_pre_ffn, g_post_ffn, w_up, conv_w_ffn, conv_b_ffn, w_down):
    f32 = np.float32
    x = np.asarray(x, f32)
    positions = np.asarray(positions, np.int32)
    w_in0 = np.asarray(w_in[0], f32)
    ar = np.arange
    kr = 1280 + ar(64)
    kr_sw = 1280 + np.concatenate([ar(32, 64), ar(0, 32)])
    shared = {}
    shared["w_ada"] = np.ascontiguousarray(np.asarray(w_ada[0], f32).reshape(16, 128, 24, 512).transpose(2, 1, 0, 3))
    shared["b_ada"] = np.asarray(b_ada, f32).reshape(1, -1)
    shared["g_pre_mix"] = np.ascontiguousarray(np.broadcast_to(np.asarray(g_pre_mix, f32).reshape(1, -1), (128, D)))
    shared["g_post_mix"] = np.ascontiguousarray(np.broadcast_to(np.asarray(g_post_mix, f32).reshape(1, -1), (128, D)))
    shared["g_pre_ffn"] = np.ascontiguousarray(np.broadcast_to(np.asarray(g_pre_ffn, f32).reshape(1, -1), (128, D)))
    shared["g_post_ffn"] = _vec_pk(np.asarray(g_post_ffn[0], f32))
    shared["w_kv"] = _chunk_cols(w_in0, np.concatenate([768 + ar(512), kr, kr_sw]))
    shared["w_q"] = _chunk_cols(w_in0, ar(768))
    conv_cols = []
    for i in range(8):
        for base in (1344, 2368, 3392):
            conv_cols.append(_chunk_cols(w_in0, base + i * 128 + ar(128)))
    shared["w_conv"] = np.stack(conv_cols)
    wuq0 = np.asarray(w_uq[0], f32)
    uq = []
    for h in range(8):
        cols = np.concatenate([h * 192 + ar(128), h * 192 + 128 + ar(64), h * 192 + 128 + np.concatenate([ar(32, 64), ar(0, 32)])])
        uq.append(_chunk_cols(wuq0, cols))
    shared["w_uq"] = np.stack(uq)
    wukv0 = np.asarray(w_ukv[0], f32)
    shared["w_uk"] = _chunk_cols(wukv0, np.concatenate([h * 256 + ar(128) for h in range(8)]))
    shared["w_uv"] = _chunk_cols(wukv0, np.concatenate([h * 256 + 128 + ar(128) for h in range(8)]))
    shared["w_o"] = _chunk_cols(np.asarray(w_o[0], f32), ar(D))
    wup0 = np.asarray(w_up[0], f32)
    ups = []
    for i in range(44):
        ups.append(_chunk_cols(wup0, i * 128 + ar(128)))
        ups.append(_chunk_cols(wup0, DFF + i * 128 + ar(128)))
    shared["w_up"] = np.stack(ups)
    wd0 = np.asarray(w_down[0], f32)
    shared["w_down"] = np.stack([_chunk_cols(wd0, oc * 128 + ar(128)) for oc in range(16)])
    shared["g_q"] = _vec_pk(np.asarray(g_q[0], f32))
    shared["g_kv"] = _vec_pk(np.asarray(g_kv[0], f32))
    cwm = np.asarray(conv_w_mix[0], f32)
    shared["cw_mix"] = np.ascontiguousarray(cwm.T.reshape(8, 128, 3).transpose(1, 0, 2))
    shared["cb_mix"] = _vec_pk(np.asarray(conv_b_mix[0], f32))
    cwf = np.asarray(conv_w_ffn[0], f32)
    shared["cw_a"] = np.ascontiguousarray(cwf[:, :DFF].T.reshape(44, 128, 3).transpose(1, 0, 2))
    shared["cw_g"] = np.ascontiguousarray(cwf[:, DFF:].T.reshape(44, 128, 3).transpose(1, 0, 2))
    cbf = np.asarray(conv_b_ffn[0], f32)
    shared["cb_a"] = _vec_pk(cbf[:DFF])
    shared["cb_g"] = _vec_pk(cbf[DFF:])
    invf = (1.0 / (10000.0 ** (np.arange(0, 64, 2, dtype=np.float32) / np.float32(64)))).astype(f32)
    shared["invf"] = np.concatenate([invf, invf]).reshape(64, 1).astype(f32)
    shared["sgn"] = np.concatenate([-np.ones(32, f32), np.ones(32, f32)]).reshape(64, 1)
    in_maps = []
    for core in range(8):
        b, j = core // 4, core % 4
        d = dict(shared)
        d["x_all"] = np.ascontiguousarray(x[b])
        xo = np.zeros((HO, D), f32)
        po = np.zeros((HO,), np.int32)
        hv = np.zeros((16,), f32)
        mh = np.zeros((128, 64, 16), f32)
        kidx = (np.arange(64)[None, :] * 128 + np.arange(128)[:, None])
        for m in range(4):
            gb = 4 * m + j
            t0 = gb * 512
            xo[16 + m * 512:16 + (m + 1) * 512] = x[b, t0:t0 + 512]
            po[16 + m * 512:16 + (m + 1) * 512] = positions[b, t0:t0 + 512]
            if gb > 0:
                xo[4 * m:4 * m + 4] = x[b, t0 - 4:t0]
                po[4 * m:4 * m + 4] = positions[b, t0 - 4:t0]
                hv[4 * m:4 * m + 4] = 1.0
                for i in range(4):
                    mh[:, :, 4 * m + i] = np.where(kidx <= t0 - 4 + i, 0.0, NEG)
        d["x_own"] = xo
        d["pos_own"] = np.ascontiguousarray(np.broadcast_to(po[None, :], (64, HO)))
        d["pos_all"] = np.ascontiguousarray(np.broadcast_to(positions[b][None, :], (64, S)))
        d["c_t"] = _vec_pk(np.asarray(c[b], f32))
        d["hval"] = np.ascontiguousarray(np.broadcast_to(hv[None, :], (128, 16)))
        kk = np.arange(16)[None, :, None] * 128 + np.arange(128)[:, None, None]
        qq = j * 512 + np.arange(512)[None, None, :]
        d["mask_own"] = np.where(kk <= qq, 0.0, NEG).astype(ml_dtypes.bfloat16)
        d["mask_halo"] = mh.astype(ml_dtypes.bfloat16)
        in_maps.append(d)
    return in_maps


def assemble(results):
    outp = np.zeros((2, S, D), np.float32)
    for core in range(8):
        b, j = core // 4, core % 4
        o = results[core]["out"]
        for m in range(4):
            gb = 4 * m + j
            outp[b, gb * 512:(gb + 1) * 512] = o[m * 512:(m + 1) * 512]
    return outp


def kernel(**inputs):
    in_maps = prepare_inputs(**inputs)
    nc, info = build()
    res = run_bass_kernel_spmd(nc, in_maps, core_ids=list(range(8)))
    return assemble(res.results)
```
